# Optimizing a Trainium2 kernel written in Bass

```python
import jax, jax.numpy as jnp
from jax import lax
import numpy as np

D_MODEL = 1024
BATCH = 16
SEQ = 2048
DEPTH = 2
DEC_BATCH = 32
DEC_SEQ = 64
PAST_LEN = 2048

N_HEADS = 16
HEAD_DIM = D_MODEL // N_HEADS
CHUNK = 64
LEFT_CHUNKS = 8
BAND = (LEFT_CHUNKS + 1) * CHUNK
A_CACHE_ROWS = LEFT_CHUNKS * CHUNK
REL_CLIP = 128
N_REL = 2 * REL_CLIP + 1
SB_BLOCK = 128
N_A_LAYERS = DEPTH // 2
N_B_LAYERS = DEPTH - N_A_LAYERS
RMS_EPS = 1e-6
NEG_INF = -1e30
SCALE = HEAD_DIM ** -0.5

kernel_name = 'yoco_chunkband_stickbreaking_step'


def rms_norm(x, g):
    xf = x.astype(jnp.float32)
    y = xf * lax.rsqrt(jnp.mean(xf * xf, axis=-1, keepdims=True) + RMS_EPS)
    return (y * g.astype(jnp.float32)).astype(x.dtype)


def split_heads(t):
    return t.reshape(*t.shape[:-1], N_HEADS, HEAD_DIM)


def gated_out(o, gate, w_out):
    b, t = o.shape[:2]
    return (o.reshape(b, t, D_MODEL) * jax.nn.silu(gate)) @ w_out


def a_project(x, g, w_in):
    h = rms_norm(x, g)
    q, k, v, gate = jnp.split(h @ w_in, 4, axis=-1)
    return split_heads(q), split_heads(k), split_heads(v), gate


def b_project(x, g, w_in):
    h = rms_norm(x, g)
    q, gate = jnp.split(h @ w_in, 2, axis=-1)
    return split_heads(q), gate


def shared_kv(x, g, w_kv):
    h = rms_norm(x, g)
    k, v = jnp.split(h @ w_kv, 2, axis=-1)
    return split_heads(k), split_heads(v)


def band_attend(q, k, v, q_pos, k_pos, rel_bias):
    s = jnp.einsum('bqhd,bkhd->bhqk', q, k).astype(jnp.float32) * SCALE
    rel = jnp.clip(q_pos[:, None] - k_pos[None, :], -REL_CLIP, REL_CLIP) + REL_CLIP
    s = s + rel_bias[:, rel].astype(jnp.float32)[None]
    qc = q_pos // CHUNK
    kc = k_pos // CHUNK
    mask = ((k_pos[None, :] >= 0) & (kc[None, :] <= qc[:, None])
            & (kc[None, :] >= qc[:, None] - LEFT_CHUNKS))
    s = jnp.where(mask[None, None], s, NEG_INF)
    p = jax.nn.softmax(s, axis=-1).astype(v.dtype)
    return jnp.einsum('bhqk,bkhd->bqhd', p, v)


def chunk_band_prompt(q, k, v, rel_bias):
    b, t, h, dh = q.shape
    n_chunks = t // CHUNK
    pad = A_CACHE_ROWS
    kp = jnp.pad(k, ((0, 0), (pad, 0), (0, 0), (0, 0)))
    vp = jnp.pad(v, ((0, 0), (pad, 0), (0, 0), (0, 0)))
    qc = q.reshape(b, n_chunks, CHUNK, h, dh).transpose(1, 0, 2, 3, 4)

    def one_chunk(args):
        c, qb = args
        start = c * CHUNK
        kb = lax.dynamic_slice_in_dim(kp, start, BAND, axis=1)
        vb = lax.dynamic_slice_in_dim(vp, start, BAND, axis=1)
        q_pos = start + jnp.arange(CHUNK, dtype=jnp.int32)
        k_pos = start - pad + jnp.arange(BAND, dtype=jnp.int32)
        return band_attend(qb, kb, vb, q_pos, k_pos, rel_bias)

    out = lax.map(one_chunk, (jnp.arange(n_chunks, dtype=jnp.int32), qc))
    return out.transpose(1, 0, 2, 3, 4).reshape(b, t, h, dh)


def sb_block(q, k, v, q_pos, k_pos):
    z = jnp.einsum('bqhd,bkhd->bhqk', q, k).astype(jnp.float32) * SCALE
    mask = (k_pos[None, :] < q_pos[:, None])[None, None]
    log_fail = jnp.where(mask, jax.nn.log_sigmoid(-z), 0.0)
    later = jnp.flip(jnp.cumsum(jnp.flip(log_fail, -1), axis=-1), -1) - log_fail
    w = jnp.where(mask, jnp.exp(jax.nn.log_sigmoid(z) + later), 0.0)
    return jnp.einsum('bhqk,bkhd->bqhd', w.astype(v.dtype), v)


def sb_prompt(q, k, v):
    t = q.shape[1]
    pos = jnp.arange(t, dtype=jnp.int32)
    outs = []
    for start in range(0, t, SB_BLOCK):
        end = start + SB_BLOCK
        outs.append(sb_block(q[:, start:end], k[:, :end], v[:, :end], pos[start:end], pos[:end]))
    return jnp.concatenate(outs, axis=1)


def setup_inputs(seed: int = 0) -> dict:
    key = jax.random.key(seed)
    ks = jax.random.split(key, 16)
    d = D_MODEL
    a_rows = min(A_CACHE_ROWS, PAST_LEN)
    nrm = jax.random.normal
    return {
        'x_prompt': nrm(ks[0], (BATCH, SEQ, d), jnp.float32),
        'x_sample': nrm(ks[1], (DEC_BATCH, DEC_SEQ, d), jnp.float32),
        'cache_a_k': nrm(ks[2], (N_A_LAYERS, DEC_BATCH, a_rows, N_HEADS, HEAD_DIM), jnp.float32),
        'cache_a_v': nrm(ks[3], (N_A_LAYERS, DEC_BATCH, a_rows, N_HEADS, HEAD_DIM), jnp.float32),
        'cache_b_k': nrm(ks[4], (DEC_BATCH, PAST_LEN, N_HEADS, HEAD_DIM), jnp.float32),
        'cache_b_v': nrm(ks[5], (DEC_BATCH, PAST_LEN, N_HEADS, HEAD_DIM), jnp.float32),
        'norm_a': 1.0 + 0.02 * nrm(ks[6], (N_A_LAYERS, d), jnp.float32),
        'w_in_a': nrm(ks[7], (N_A_LAYERS, d, 4 * d), jnp.float32) * d ** -0.5,
        'rel_bias_a': 0.1 * nrm(ks[8], (N_A_LAYERS, N_HEADS, N_REL), jnp.float32),
        'w_out_a': nrm(ks[9], (N_A_LAYERS, d, d), jnp.float32) * d ** -0.5,
        'norm_kv': 1.0 + 0.02 * nrm(ks[10], (d,), jnp.float32),
        'w_kv': nrm(ks[11], (d, 2 * d), jnp.float32) * d ** -0.5,
        'norm_b': 1.0 + 0.02 * nrm(ks[12], (N_B_LAYERS, d), jnp.float32),
        'w_in_b': nrm(ks[13], (N_B_LAYERS, d, 2 * d), jnp.float32) * d ** -0.5,
        'w_out_b': nrm(ks[14], (N_B_LAYERS, d, d), jnp.float32) * d ** -0.5,
        'norm_f': 1.0 + 0.02 * nrm(ks[15], (d,), jnp.float32),
    }


def reference(x_prompt, x_sample, cache_a_k, cache_a_v, cache_b_k, cache_b_v,
              norm_a, w_in_a, rel_bias_a, w_out_a, norm_kv, w_kv,
              norm_b, w_in_b, w_out_b, norm_f):
    past_len = cache_b_k.shape[1]
    ts = x_sample.shape[1]
    a_rows = cache_a_k.shape[2]
    q_pos_s = past_len + jnp.arange(ts, dtype=jnp.int32)
    k_pos_a = past_len - a_rows + jnp.arange(a_rows + ts, dtype=jnp.int32)
    k_pos_b = jnp.arange(past_len + ts, dtype=jnp.int32)

    xp, xs = x_prompt, x_sample
    a_kp, a_vp, a_ks, a_vs = [], [], [], []
    for layer in range(DEPTH):
        if layer < N_A_LAYERS:
            i = layer
            qp, kp, vp, gp = a_project(xp, norm_a[i], w_in_a[i])
            qs, ks, vs, gs = a_project(xs, norm_a[i], w_in_a[i])
            op = chunk_band_prompt(qp, kp, vp, rel_bias_a[i])
            k_all = jnp.concatenate([cache_a_k[i], ks], axis=1)
            v_all = jnp.concatenate([cache_a_v[i], vs], axis=1)
            o_s = band_attend(qs, k_all, v_all, q_pos_s, k_pos_a, rel_bias_a[i])
            xp = xp + gated_out(op, gp, w_out_a[i])
            xs = xs + gated_out(o_s, gs, w_out_a[i])
            keep = min(A_CACHE_ROWS, xp.shape[1])
            a_kp.append(kp[:, -keep:])
            a_vp.append(vp[:, -keep:])
            a_ks.append(ks)
            a_vs.append(vs)
        else:
            if layer == N_A_LAYERS:
                kb_p, vb_p = shared_kv(xp, norm_kv, w_kv)
                kb_s, vb_s = shared_kv(xs, norm_kv, w_kv)
                kb_all = jnp.concatenate([cache_b_k, kb_s], axis=1)
                vb_all = jnp.concatenate([cache_b_v, vb_s], axis=1)
            j = layer - N_A_LAYERS
            qp, gp = b_project(xp, norm_b[j], w_in_b[j])
            qs, gs = b_project(xs, norm_b[j], w_in_b[j])
            op = sb_prompt(qp, kb_p, vb_p)
            o_s = sb_block(qs, kb_all, vb_all, q_pos_s, k_pos_b)
            xp = xp + gated_out(op, gp, w_out_b[j])
            xs = xs + gated_out(o_s, gs, w_out_b[j])

    y_prompt = rms_norm(xp, norm_f)
    y_sample = rms_norm(xs, norm_f)
    new_a_k_prompt = jnp.stack(a_kp)
    new_a_v_prompt = jnp.stack(a_vp)
    new_a_k_sample = jnp.stack(a_ks)
    new_a_v_sample = jnp.stack(a_vs)
    return (y_prompt, y_sample, new_a_k_prompt, new_a_v_prompt, kb_p, vb_p,
            new_a_k_sample, new_a_v_sample, kb_s, vb_s)
```

```python
import numpy as np
from contextlib import ExitStack
import concourse.bass as bass
import concourse.mybir as mybir
from concourse.bass_utils import run_bass_kernel_spmd

F32 = mybir.dt.float32
BF16 = mybir.dt.bfloat16
AF = mybir.ActivationFunctionType
ALU = mybir.AluOpType

D = 1024
H = 16
DH = 64
KC = 8
SCALE = DH ** -0.5
NEG = -30000.0
EPS = 1e-6
NCORES = 8


class Ins:
    __slots__ = ("eng", "fn", "idx", "waits", "signal", "dma", "sem", "val", "count")

    def __init__(self, eng, fn, idx, dma):
        self.eng = eng; self.fn = fn; self.idx = idx; self.dma = dma
        self.waits = []; self.signal = False; self.sem = None; self.val = None; self.count = None


class Sched:
    ENGS = ("pe", "act", "dve", "pool", "sp")
    NDMA = 16

    def __init__(self):
        self.streams = {e: [] for e in self.ENGS}
        self.last_w = {}
        self.readers = {}
        self.waited = {e: {f: -1 for f in self.ENGS} for e in self.ENGS}
        self.waited_dma = {e: set() for e in self.ENGS}
        self.dma_list = {"sp": [], "pool": []}
        self.live_dma = []

    def _resolve(self, ins, deps):
        eng = ins.eng
        for p in deps:
            if p is ins or p is None:
                continue
            if p.dma:
                if id(p) in self.waited_dma[eng]:
                    continue
                self.waited_dma[eng].add(id(p))
                ins.waits.append(p)
            else:
                if p.eng == eng and eng == "pe":
                    continue
                if self.waited[eng][p.eng] >= p.idx:
                    continue
                self.waited[eng][p.eng] = p.idx
                p.signal = True
                ins.waits.append(p)

    def add(self, eng, fn, reads=(), writes=(), dma=False):
        st = self.streams[eng]
        ins = Ins(eng, fn, len(st), dma)
        ps_reads = [r for r in reads if isinstance(r, tuple) and r[0] in ("psg", "pst", "pss", "psc")]
        if ps_reads:
            writes = list(writes) + [r for r in ps_reads if r not in writes]
        deps = []
        for r in reads:
            lw = self.last_w.get(r)
            if lw is not None:
                deps.append(lw)
        for w in writes:
            lw = self.last_w.get(w)
            if lw is not None:
                deps.append(lw)
            deps.extend(self.readers.get(w, ()))
        if dma:
            q = self.dma_list[eng]
            j = len(q)
            ins.sem = (eng, j % self.NDMA)
            ins.val = 16 * (j // self.NDMA + 1)
            if j >= self.NDMA:
                deps.append(q[j - self.NDMA])
            q.append(ins)
            self.live_dma.append(ins)
        self._resolve(ins, deps)
        for r in reads:
            lst = self.readers.setdefault(r, [])
            if not dma:
                lst[:] = [x for x in lst if x.dma or x.eng != eng]
            lst.append(ins)
        for w in writes:
            self.last_w[w] = ins
            self.readers[w] = []
        st.append(ins)
        return ins

    def barrier(self):
        lasts = {}
        for e in self.ENGS:
            real = [x for x in self.streams[e] if x.fn is not None and not x.dma]
            lasts[e] = real[-1] if real else None
        dmas = list(self.live_dma)
        for e in self.ENGS:
            ins = Ins(e, None, len(self.streams[e]), False)
            deps = [lasts[f] for f in self.ENGS if f != e and lasts[f] is not None] + dmas
            self._resolve(ins, deps)
            self.streams[e].append(ins)
        self.live_dma = []
        self.last_w = {}
        self.readers = {}

    def emit(self, sems, dsems, block):
        for e in self.ENGS:
            c = 0
            for ins in self.streams[e]:
                if ins.dma or ins.fn is None:
                    continue
                if ins.signal:
                    c += 1
                    ins.count = c

        def run(engname, eng):
            for ins in self.streams[engname]:
                best = {}
                for p in ins.waits:
                    if p.dma:
                        key = ("d",) + p.sem
                        v = p.val
                        s = dsems[p.sem[0]][p.sem[1]]
                    else:
                        key = ("e", p.eng)
                        v = p.count
                        s = sems[p.eng]
                    if key not in best or best[key][1] < v:
                        best[key] = (s, v)
                for s, v in best.values():
                    eng.wait_ge(s, v)
                if ins.fn is None:
                    continue
                bi = ins.fn(eng)
                if ins.dma:
                    bi.then_inc(dsems[ins.sem[0]][ins.sem[1]], 16)
                elif ins.signal:
                    bi.then_inc(sems[engname], 1)
            if engname in self.dma_list:
                last = {}
                for ins in self.dma_list[engname]:
                    last[ins.sem] = ins.val
                for sm, v in last.items():
                    eng.wait_ge(dsems[sm[0]][sm[1]], v)

        @block.tensor
        def _(eng):
            run("pe", eng)

        @block.scalar
        def _(eng):
            run("act", eng)

        @block.vector
        def _(eng):
            run("dve", eng)

        @block.gpsimd
        def _(eng):
            run("pool", eng)

        @block.sync
        def _(eng):
            run("sp", eng)


def build_program(SEQ, PAST, NPASS):
    NT = SEQ // 128
    NST = SEQ // 512
    NKB = PAST // 128
    NKBT = max(NT, NKB + 1)
    KTN = NKBT * 128
    NS = 2 * NPASS
    nc = bass.Bass("TRN2", target_bir_lowering=False)

    def din(name, shape):
        return nc.dram_tensor(name, shape, F32, kind="ExternalInput").ap()

    def dout(name, shape):
        return nc.dram_tensor(name, shape, F32, kind="ExternalOutput").ap()

    xp = din("xp", [NPASS, SEQ, D])
    xs = din("xs", [NS, 64, D])
    cak = din("cak", [NS, 512, D])
    cav = din("cav", [NS, 512, D])
    cbk = din("cbk", [NS, PAST, D])
    cbv = din("cbv", [NS, PAST, D])
    w_in_a = din("w_in_a", [D, 4 * D])
    w_out_a = din("w_out_a", [D, D])
    w_kv = din("w_kv", [D, 2 * D])
    w_in_b = din("w_in_b", [D, 2 * D])
    w_out_b = din("w_out_b", [D, D])
    g128 = din("g128", [4, 128, D])
    ttab = din("ttab", [H, 128, 640])
    cst = din("cst", [128, 896])
    cst2 = din("cst2", [128, 704])
    y_p = dout("y_p", [NPASS, SEQ, D])
    y_s = dout("y_s", [NS, 64, D])
    nak_p = dout("nak_p", [NPASS, 512, D])
    nav_p = dout("nav_p", [NPASS, 512, D])
    nbk_p = dout("nbk_p", [NPASS, SEQ, D])
    nbv_p = dout("nbv_p", [NPASS, SEQ, D])
    nak_s = dout("nak_s", [NS, 64, D])
    nav_s = dout("nav_s", [NS, 64, D])
    nbk_s = dout("nbk_s", [NS, 64, D])
    nbv_s = dout("nbv_s", [NS, 64, D])
    x1p = nc.dram_tensor("x1p", [NPASS, SEQ, D], F32).ap()
    x1s = nc.dram_tensor("x1s", [NS, 128, D], F32).ap()

    S = Sched()
    with ExitStack() as es:
        def sb(name, shape, dt):
            return es.enter_context(nc.sbuf_tensor(name, shape, dt))

        def ps(name, shape, dt):
            return es.enter_context(nc.psum_tensor(name, shape, dt))

        W = sb("W", [128, KC, 5 * D], BF16)
        KTF = sb("KTF", [128, max(KC * KTN, 16384)], BF16)
        VVB = sb("VVB", [128, max(NKBT * D, 16512)], BF16)
        xt = [sb(f"xt{i}", [128, D], F32) for i in range(2)]
        hb = [sb(f"hb{i}", [128, D], BF16) for i in range(2)]
        hTf = sb("hTf", [128, KC * 512], BF16)
        QTf = sb("QTf", [128, KC * 512], BF16)
        hkT = sb("hkT", [128, KC, 256], BF16)
        ogT = sb("ogT", [128, KC, 128], BF16)
        TTb = sb("TTb", [128, 1280], F32)
        F01 = sb("F01", [128, 1024], F32)
        gbc0 = sb("gbc0", [128, D], F32)
        stage = [sb(f"stage{i}", [128, D], F32) for i in range(2)]
        ident = sb("ident", [128, 128], BF16)
        Lm = sb("Lm", [128, 128], BF16)
        m01 = sb("m01", [128, 512], BF16)
        Jm = sb("Jm", [128, 128], BF16)
        Zm = sb("Zm", [128, 128], BF16)
        LmR = sb("LmR", [128, 128], BF16)
        m01R = sb("m01R", [128, 512], BF16)
        small = sb("small", [128, 16], F32)
        KTB = KTF[:, 0:KC * KTN].rearrange("p (k n) -> p k n", k=KC)
        KTAv = KTF[:, 0:KC * 1024].rearrange("p (k n) -> p k n", k=KC)
        ogA = KTF[:, 8192:12288].rearrange("p (m n) -> p m n", m=4)
        sgA = KTF[:, 12288:16384].rearrange("p (m n) -> p m n", m=4)
        hT = hTf[:, :].rearrange("p (k n) -> p k n", k=KC)
        QT = QTf[:, :].rearrange("p (k n) -> p k n", k=KC)
        PTA = [VVB[:, 8320 + i * 4096: 8320 + (i + 1) * 4096].rearrange("p (k n) -> p k n", k=8) for i in range(2)]
        PBt = hTf[:, 0:3072].rearrange("p (k n) -> p k n", k=6)
        nQT = hTf[:, 3072:4096].rearrange("p (k n) -> p k n", k=KC)
        gbc = [gbc0, QTf[:, 0:2048].bitcast(F32), QTf[:, 2048:4096].bitcast(F32)]
        TT = [TTb[:, 0:640], TTb[:, 640:1280]]
        F = [F01[:, 0:512], F01[:, 512:1024], TTb[:, 0:512], TTb[:, 512:1024]]
        F6 = [F01[:, 0:512], F01[:, 512:1024], TTb[:, 0:512], TTb[:, 512:1024], stage[1][:, 0:512], stage[1][:, 512:1024]]
        carryT = hkT[0:1, :, :].rearrange("p k n -> p (k n)")
        ktok = hb[0]
        cstf = stage[0]
        psg = [ps(f"psg{i}", [128, 512], F32) for i in range(2)]
        pst = [ps(f"pst{i}", [128, 1024], BF16) for i in range(2)]
        pss = [ps(f"pss{i}", [128, 512], F32) for i in range(2)]
        psc = [ps(f"psc{i}", [128, 512], F32) for i in range(2)]

        sems = {e: es.enter_context(nc.semaphore("s_" + e)) for e in Sched.ENGS}
        dsems = {q: [es.enter_context(nc.semaphore(f"d_{q}{i}")) for i in range(Sched.NDMA)]
                 for q in ("sp", "pool")}
        block = es.enter_context(nc.Block())

        KTA = KTAv
        VA = VVB[:, 0:8 * H * 65].rearrange("p (k h e) -> p k h e", k=8, h=H)
        VB = VVB[:, 0:NKBT * D].rearrange("p (k n) -> p k n", k=NKBT)
        HI = [PBt[:, 0 + i, :] for i in range(2)]
        LO = [PBt[:, 2 + i, :] for i in range(2)]
        PB = [PBt[:, 4 + i, :] for i in range(2)]
        ss = small[:, 0:1]
        lnv = small[:, 1:2]
        rstd = small[:, 2:3]
        rc = small[:, 4:8]
        epsb = small[:, 8:9]
        oh127 = small[:, 9:10]

        def dma(q, out, in_, reads=(), writes=()):
            S.add(q, lambda e: e.dma_start(out=out, in_=in_), reads=reads, writes=writes, dma=True)

        def load_w(dst_col0, src, ncols, keys):
            srcv = src.rearrange("(kc p) n -> p kc n", p=128)
            for c in range(0, ncols, 512):
                key = keys[(c // 1024)]
                dma("pool", W[:, :, dst_col0 + c:dst_col0 + c + 512], srcv[:, :, c:c + 512],
                    writes=[key])

        def rms_stats(xtile, xkey, alt=False):
            ss_, ln_, rs_ = (small[:, 10:11], small[:, 11:12], small[:, 12:13]) if alt else (ss, lnv, rstd)
            tg = "2" if alt else ""
            S.add("act", lambda e: e.activation(out=ogT[:, :, :].rearrange("p k n -> p (k n)"), in_=xtile[:],
                                                func=AF.Square, accum_out=ss_),
                  reads=[xkey], writes=["ogT", "ss" + tg])
            S.add("act", lambda e: e.activation(out=ln_, in_=ss_, func=AF.Ln, scale=1.0 / D, bias=epsb),
                  reads=["ss" + tg, "epsb"], writes=["lnv" + tg])
            S.add("act", lambda e: e.activation(out=rs_, in_=ln_, func=AF.Exp, scale=-0.5),
                  reads=["lnv" + tg], writes=["rstd" + tg])

        def norm_to(xtile, xkey, g, gkey, dst, dkey, alt=False):
            rs_ = small[:, 12:13] if alt else rstd
            rk = "rstd2" if alt else "rstd"
            S.add("dve", lambda e: e.scalar_tensor_tensor(out=dst, in0=xtile[:], scalar=rs_, in1=g[:],
                                                          op0=ALU.mult, op1=ALU.mult),
                  reads=[xkey, rk, gkey], writes=[dkey])

        tcount = [0]

        def transpose_tile(src, skey, dst, dkey, evac_eng):
            b = tcount[0] % 2
            tcount[0] += 1
            for k in range(KC):
                S.add("pe", lambda e, k=k: e.transpose(out=pst[b][:, k * 128:(k + 1) * 128],
                                                       in_=src[:, k * 128:(k + 1) * 128], identity=ident[:]),
                      reads=[skey, "ident"], writes=[("pst", b)])
            pv = pst[b][:, :].rearrange("p (k n) -> p k n", k=KC)
            if evac_eng == "act":
                S.add("act", lambda e: e.copy(out=dst, in_=pv), reads=[("pst", b)], writes=[dkey])
            else:
                S.add("dve", lambda e: e.tensor_copy(out=dst, in_=pv), reads=[("pst", b)], writes=[dkey])

        gcount = [0]

        def jtranspose_tile(src, skey, dst, dkey):
            for half in range(2):
                b = gcount[0] % 2
                gcount[0] += 1
                for k4 in range(4):
                    k = half * 4 + k4
                    S.add("pe", lambda e, b=b, k=k, k4=k4: e.matmul(psg[b][:, k4 * 128:(k4 + 1) * 128],
                                                                  lhsT=src[:, k * 128:(k + 1) * 128], rhs=Jm[:],
                                                                  start=True, stop=True),
                          reads=[skey, "Jm"], writes=[("psg", b)])
                pv = psg[b][:, :].rearrange("p (k n) -> p k n", k=4)
                if half == 0:
                    S.add("act", lambda e, pv=pv, half=half: e.copy(out=dst[:, half * 4:(half + 1) * 4, :], in_=pv),
                          reads=[("psg", b)], writes=[dkey])
                else:
                    S.add("dve", lambda e, pv=pv, half=half: e.tensor_copy(out=dst[:, half * 4:(half + 1) * 4, :], in_=pv),
                          reads=[("psg", b)], writes=[dkey])

        def jreverse_rows(vt, vkey):
            bs = []
            for ncn in range(2):
                b = gcount[0] % 2
                gcount[0] += 1
                bs.append(b)
                S.add("pe", lambda e, b=b, ncn=ncn: e.matmul(psg[b][:, :], lhsT=Jm[:], rhs=vt[:, ncn * 512:(ncn + 1) * 512],
                                                          start=True, stop=True),
                      reads=[vkey, "Jm"], writes=[("psg", b)])
            for ncn in range(2):
                b = bs[ncn]
                eng = "act" if ncn == 0 else "dve"
                if eng == "act":
                    S.add("act", lambda e, b=b, ncn=ncn: e.copy(out=vt[:, ncn * 512:(ncn + 1) * 512], in_=psg[b][:, :]),
                          reads=[("psg", b)], writes=[vkey])
                else:
                    S.add("dve", lambda e, b=b, ncn=ncn: e.tensor_copy(out=vt[:, ncn * 512:(ncn + 1) * 512], in_=psg[b][:, :]),
                          reads=[("psg", b)], writes=[vkey])

        def gemm_fm(wcol, wkey, rhsT, rkey, n, ncol_groups=1):
            b = gcount[0] % 2
            gcount[0] += 1
            for g in range(ncol_groups):
                for k in range(KC):
                    S.add("pe", lambda e, k=k, g=g: e.matmul(psg[b][:, g * n:(g + 1) * n],
                                                             lhsT=W[:, k, wcol + g * 128:wcol + (g + 1) * 128],
                                                             rhs=rhsT[:, k, 0:n], start=(k == 0), stop=(k == KC - 1)),
                          reads=[wkey, rkey], writes=[("psg", b)])
            return b

        def gemm_tm(lhsT_t, lkey, tok0, wcol, wkey):
            b = gcount[0] % 2
            gcount[0] += 1
            for k in range(KC):
                S.add("pe", lambda e, k=k: e.matmul(psg[b][:, :], lhsT=lhsT_t[:, k, tok0:tok0 + 128],
                                                    rhs=W[:, k, wcol:wcol + 512], start=(k == 0), stop=(k == KC - 1)),
                      reads=[wkey, lkey], writes=[("psg", b)])
            return b

        ecount = [0]

        def evac(out, okey, b, n=512, scale=None, eng=None, func=None, extra_reads=(), groups=None):
            if eng is None:
                eng = "act" if ecount[0] % 2 == 0 else "dve"
                ecount[0] += 1
            src = psg[b][:, 0:n]
            if groups is not None:
                src = psg[b][:, 0:n].rearrange("p (g n) -> p g n", g=groups)
            okeys = okey if isinstance(okey, list) else [okey]
            if func is not None:
                S.add("act", lambda e: e.activation(out=out, in_=src, func=func),
                      reads=[("psg", b)] + list(extra_reads), writes=okeys)
            elif eng == "act":
                if scale is None:
                    S.add("act", lambda e: e.copy(out=out, in_=src), reads=[("psg", b)], writes=okeys)
                else:
                    S.add("act", lambda e: e.activation(out=out, in_=src, func=AF.Copy, scale=scale),
                          reads=[("psg", b)], writes=okeys)
            else:
                if scale is None:
                    S.add("dve", lambda e: e.tensor_copy(out=out, in_=src), reads=[("psg", b)], writes=okeys)
                else:
                    S.add("dve", lambda e: e.tensor_scalar(out=out, in0=src, scalar1=scale, scalar2=None,
                                                           op0=ALU.mult),
                          reads=[("psg", b)], writes=okeys)

        scount = [0]

        def stage_out(dst_dram, b0, rows=128):
            pass

        dma("sp", cstf[:, 0:896], cst[:, :], writes=["stage0"])
        dma("sp", stage[1][:, 0:704], cst2[:, :], writes=["stage1"])
        S.add("dve", lambda e: e.tensor_copy(out=ident[:], in_=cstf[:, 0:128]), reads=["stage0"], writes=["ident"])
        S.add("dve", lambda e: e.tensor_copy(out=Lm[:], in_=cstf[:, 128:256]), reads=["stage0"], writes=["Lm"])
        S.add("dve", lambda e: e.tensor_copy(out=m01[:], in_=cstf[:, 256:768]), reads=["stage0"], writes=["m01"])
        S.add("dve", lambda e: e.tensor_copy(out=Jm[:], in_=cstf[:, 768:896]), reads=["stage0"], writes=["Jm"])
        S.add("dve", lambda e: e.tensor_copy(out=LmR[:], in_=stage[1][:, 0:128]), reads=["stage1"], writes=["LmR"])
        S.add("dve", lambda e: e.tensor_copy(out=m01R[:], in_=stage[1][:, 128:640]), reads=["stage1"], writes=["m01R"])
        S.add("dve", lambda e: e.tensor_copy(out=oh127, in_=stage[1][:, 640:641]), reads=["stage1"], writes=["oh127"])
        S.add("dve", lambda e: e.memset(epsb, EPS), writes=["epsb"])
        S.add("dve", lambda e: e.memset(Zm[:], 0.0), writes=["Zm"])

        WK = [("W", i) for i in range(5)]

        KSTOP = 0
        POOLENG = "dve"
        JUNK = 0

        class _Stop(Exception):
            pass

        ui_box = [0]

        def chk(n_):
            if KSTOP == ui_box[0] * 100 + n_:
                raise _Stop()

        def body():
          for p in range(NPASS):
              S.barrier()
              if p == 0:
                  load_w(0, w_in_a[:, :], 4 * D, WK[0:4])
                  load_w(4 * D, w_out_a[:, :], D, WK[4:5])
              dma("sp", gbc[0][:], g128[0], writes=["gbc0"])
              S.add("pool", lambda e: e.memset(KTF[:, 0:KC * 1024], 0.0), writes=[("KTA", s) for s in range(8)])
              S.add("pool", lambda e: e.memset(VVB[:, 0:8 * H * 65], 0.0), writes=[("VA", s) for s in range(8)])
              S.add("pool", lambda e: e.memset(VA[:, :, :, 64:65], 1.0), writes=[("VA", s) for s in range(8)])
              for i in range(2):
                  S.add("pool", lambda e, i=i: e.memset(VVB[:, 8320 + i * 4096: 8320 + (i + 1) * 4096], 0.0),
                        writes=[("PT", i)])
              S.add("pool", lambda e: e.memset(KTF[:, 8192:16384], 0.0), writes=["og", "sg"])
              chk(1)

              units = [("p", st) for st in range(NST)] + [("s", 2 * p), ("s", 2 * p + 1)]
              for ui, unit in enumerate(units):
                  ui_box[0] = ui
                  is_s = unit[0] == "s"
                  ntt = 1 if is_s else 4
                  n = ntt * 128
                  if is_s:
                      su = unit[1]
                      curh, prevh = 1, 0
                      kbs_avail = [0, 1, 2, 3, 4]
                      qchunks = [8]
                      want_out = True
                  else:
                      st = unit[1]
                      curh, prevh = st % 2, 1 - st % 2
                      kbs_avail = list(range(4, 8)) if st == 0 else list(range(8))
                      qchunks = list(range(8, 16))
                      want_out = (st == NST - 1)

                  def slot(kb):
                      return (prevh * 4 + kb) if kb < 4 else (curh * 4 + kb - 4)

                  def xsrc(tt):
                      if is_s:
                          return xs[su], 64
                      return xp[p, (st * 4 + tt) * 128:(st * 4 + tt + 1) * 128, :], 128

                  if is_s:
                      for kb in range(4):
                          dma("pool", ktok[:], cak[su, kb * 128:(kb + 1) * 128, :], writes=[("hb", 0)])
                          dstv = KTA[:, :, slot(kb) * 128:(slot(kb) + 1) * 128]
                          transpose_tile(ktok, ("hb", 0), dstv, ("KTA", slot(kb)), "act")
                          dma("pool", VA[:, slot(kb), :, 0:64],
                              cav[su, kb * 128:(kb + 1) * 128, :].rearrange("t (h e) -> t h e", h=H),
                              writes=[("VA", slot(kb))])

                  def prologue_tile(unit_, tt_):
                      if unit_[0] == "s":
                          src_, rows_ = xs[unit_[1]], 64
                      else:
                          src_, rows_ = xp[p, (unit_[1] * 4 + tt_) * 128:(unit_[1] * 4 + tt_ + 1) * 128, :], 128
                      xb_ = tt_ % 2
                      if rows_ < 128:
                          S.add("pool", lambda e, xb_=xb_: e.memset(xt[xb_][64:128, :], 0.0), writes=[("xt", xb_)])
                      dma("sp", xt[xb_][0:rows_, :], src_, writes=[("xt", xb_)])
                      rms_stats(xt[xb_], ("xt", xb_))
                      norm_to(xt[xb_], ("xt", xb_), gbc[0], "gbc0", hb[xb_][:], ("hb", xb_))
                      transpose_tile(hb[xb_], ("hb", xb_), hT[:, :, tt_ * 128:(tt_ + 1) * 128], "hT", "act")

                  if ui == 0:
                      for tt in range(ntt):
                          prologue_tile(unit, tt)
                  nxt = units[ui + 1] if ui + 1 < len(units) else None
                  nxt_ntt = 0 if nxt is None else (1 if nxt[0] == "s" else 4)
                  chk(2)

                  for fp in range(8):
                      b = gemm_fm(fp * 128, WK[0], hT, "hT", n)
                      evac(QT[:, fp, 0:n], "QT", b, n)
                  for fp in range(8):
                      b = gemm_fm(D + fp * 128, WK[1], hT, "hT", n)
                      evac(KTA[:, fp, curh * 512:curh * 512 + n], [("KTA", curh * 4 + q_) for q_ in range(4)], b, n)
                  chk(3)

                  for tt in range(ntt):
                      sl = curh * 4 + tt
                      for ncn in range(2):
                          b = gemm_tm(hT, "hT", tt * 128, 2 * D + ncn * 512, WK[2])
                          S.add("dve", lambda e, b=b, sl=sl, ncn=ncn: e.tensor_copy(
                              out=VA[:, sl, ncn * 8:(ncn + 1) * 8, 0:64],
                              in_=psg[b][:, :].rearrange("p (h e) -> p h e", h=8)),
                              reads=[("psg", b)], writes=[("VA", sl)])
                          if want_out:
                              S.add("act", lambda e, b=b, ncn=ncn: e.copy(out=stage[0][:, ncn * 512:(ncn + 1) * 512],
                                                                       in_=psg[b][:, :]),
                                    reads=[("psg", b)], writes=["stage0"])
                      if want_out:
                          if is_s:
                              dma("sp", nav_s[su], stage[0][0:64, :], reads=["stage0"])
                          else:
                              dma("sp", nav_p[p, tt * 128:(tt + 1) * 128, :], stage[0][:], reads=["stage0"])
                          for ncn in range(2):
                              b = gemm_tm(hT, "hT", tt * 128, D + ncn * 512, WK[1])
                              evac(stage[0][:, ncn * 512:(ncn + 1) * 512], "stage0", b, eng="act")
                          if is_s:
                              dma("sp", nak_s[su], stage[0][0:64, :], reads=["stage0"])
                          else:
                              dma("sp", nak_p[p, tt * 128:(tt + 1) * 128, :], stage[0][:], reads=["stage0"])
                      for ncn in range(2):
                          b = gemm_tm(hT, "hT", tt * 128, 3 * D + ncn * 512, WK[3])
                          evac(sgA[:, tt, ncn * 512:(ncn + 1) * 512], "sg", b, func=AF.Silu)
                  chk(4)
                  if ui == len(units) - 1:
                      load_w(0, w_kv[:, :], 2 * D, WK[0:2])
                      load_w(2 * D, w_in_b[:, :], 2 * D, WK[2:4])

                  npair = 1 if is_s else 4
                  def st1(h):
                      tb = h % 2
                      fp = h // 2
                      r0 = (h % 2) * 64
                      dma("sp", TT[tb][:], ttab[h], writes=[("TT", tb)])
                      pb = h % 2
                      for kb in kbs_avail:
                          cs = [c for c in qchunks if 2 * kb <= c <= 2 * kb + 9]
                          if not cs:
                              continue
                          c0, c1 = cs[0], cs[-1]
                          ncol = (c1 - c0 + 1) * 64
                          q0 = (c0 - 8) * 64
                          d0 = c0 - 2 * kb
                          sl = slot(kb)
                          sbk = scount[0] % 2
                          scount[0] += 1
                          S.add("pe", lambda e, sbk=sbk, sl=sl, ncol=ncol, q0=q0, fp=fp, r0=r0: e.matmul(
                              pss[sbk][:, 0:ncol], lhsT=KTA[r0:r0 + 64, fp, sl * 128:(sl + 1) * 128],
                              rhs=QT[r0:r0 + 64, fp, q0:q0 + ncol], start=True, stop=True),
                              reads=[("KTA", sl), "QT"], writes=[("pss", sbk)])
                          S.add("dve", lambda e, sbk=sbk, ncol=ncol, d0=d0, tb=tb: e.scalar_tensor_tensor(
                              out=F[sbk][:, 0:ncol], in0=pss[sbk][:, 0:ncol], scalar=SCALE,
                              in1=TT[tb][:, d0 * 64:d0 * 64 + ncol], op0=ALU.mult, op1=ALU.add),
                              reads=[("pss", sbk), ("TT", tb)], writes=[("F", sbk)])
                          S.add("act", lambda e, sbk=sbk, ncol=ncol, q0=q0, kb=kb, pb=pb: e.activation(
                              out=PTA[pb][:, kb, q0:q0 + ncol], in_=F[sbk][:, 0:ncol], func=AF.Exp),
                              reads=[("F", sbk)], writes=[("PT", pb)])

                  def st2(h):
                      pb = h % 2
                      ob = h % 2
                      for m in range(npair):
                          kl = [kb for kb in range(m, m + 5) if kb in kbs_avail]
                          for j, kb in enumerate(kl):
                              sl = slot(kb)
                              if is_s:
                                  S.add("pe", lambda e, ob=ob, kb=kb, sl=sl, h=h, j=j, kl=kl, pb=pb: e.matmul(
                                      psc[ob][0:64, 0:65], lhsT=PTA[pb][:, kb, 0:64], rhs=VA[:, sl, h, :],
                                      start=(j == 0), stop=(j == len(kl) - 1)),
                                      reads=[("PT", pb), ("VA", sl)], writes=[("psc", ob)])
                              else:
                                  S.add("pe", lambda e, ob=ob, kb=kb, sl=sl, h=h, j=j, kl=kl, m=m, pb=pb: e.matmul(
                                      psc[ob][:, m * 65:(m + 1) * 65], lhsT=PTA[pb][:, kb, m * 128:(m + 1) * 128],
                                      rhs=VA[:, sl, h, :], start=(j == 0), stop=(j == len(kl) - 1)),
                                      reads=[("PT", pb), ("VA", sl)], writes=[("psc", ob)])
                      rows = 64 if is_s else 128
                      pv = psc[ob][0:rows, 0:npair * 65].rearrange("p (m e) -> p m e", m=npair)
                      S.add("dve", lambda e, pv=pv, rows=rows, npair=npair: e.reciprocal(out=rc[0:rows, 0:npair], in_=pv[:, :, 64]),
                            reads=[("psc", ob)], writes=["rc"])
                      for m in range(npair):
                          S.add("dve", lambda e, pv=pv, m=m, h=h, rows=rows: e.scalar_tensor_tensor(
                              out=ogA[0:rows, m, h * 64:(h + 1) * 64], in0=pv[:, m, 0:64], scalar=rc[0:rows, m:m + 1],
                              in1=sgA[0:rows, m, h * 64:(h + 1) * 64], op0=ALU.mult, op1=ALU.mult),
                              reads=[("psc", ob), "rc", "sg"], writes=["og"])

                  st1(0)
                  for h in range(H):
                      if h + 1 < H:
                          st1(h + 1)
                      st2(h)
                      if h % 4 == 3 and h // 4 < nxt_ntt:
                          prologue_tile(nxt, h // 4)
                  if is_s:
                      S.add("dve", lambda e: e.memset(ogA[64:128, 0, :], 0.0), writes=["og"])
                  chk(5)

                  for tt in range(ntt):
                      transpose_tile(ogA[:, tt, :], "og", ogT[:, :, :], "ogT", "act")
                      if is_s:
                          chk(51)
                      xb = tt % 2
                      src, rows = xsrc(tt)
                      if rows < 128:
                          S.add("pool", lambda e, xb=xb: e.memset(xt[xb][64:128, :], 0.0), writes=[("xt", xb)])
                      dma("sp", xt[xb][0:rows, :], src, writes=[("xt", xb)])
                      for ncn in range(2):
                          b = gemm_tm(ogT, "ogT", 0, 4 * D + ncn * 512, WK[4])
                          S.add("dve", lambda e, b=b, ncn=ncn, xb=xb: e.tensor_tensor(
                              out=stage[1][:, ncn * 512:(ncn + 1) * 512], in0=psg[b][:, :],
                              in1=xt[xb][:, ncn * 512:(ncn + 1) * 512], op=ALU.add),
                              reads=[("psg", b), ("xt", xb)], writes=["stage1"])
                      if is_s:
                          chk(52)
                      if is_s:
                          dma("sp", x1s[su], stage[1][:], reads=["stage1"])
                          chk(53)
                      else:
                          dma("sp", x1p[p, (st * 4 + tt) * 128:(st * 4 + tt + 1) * 128, :], stage[1][:],
                              reads=["stage1"])
                  chk(60)
                  if ui == len(units) - 1:
                      load_w(4 * D, w_out_b[:, :], D, WK[4:5])

              ui_box[0] = 9
              chk(6)
              S.barrier()
              for i in range(3):
                  dma("sp", gbc[i][:], g128[1 + i], writes=[f"gbc{i}"])
              chk(7)

              hkvT = hkT[:, :, 0:128]
              hbT = hkT[:, :, 128:256]
              unitsB = [("p", i) for i in range(NT)] + [("s", 2 * p), ("s", 2 * p + 1)]
              ccount = [0]
              for uiB, unit in enumerate(unitsB):
                  ui_box[0] = 10 + uiB
                  is_s = unit[0] == "s"
                  if is_s:
                      su = unit[1]
                      qi = NKB
                      rows = 64
                      x1src = x1s[su]
                      for kb in range(NKB):
                          kbuf = kb % 2
                          dma("pool", hb[kbuf][:], cbk[su, kb * 128:(kb + 1) * 128, :], writes=[("hb", kbuf)])
                          if kb % 2 == 0:
                              transpose_tile(hb[kbuf], ("hb", kbuf), KTB[:, :, kb * 128:(kb + 1) * 128], ("KT", kb), "act")
                          else:
                              jtranspose_tile(hb[kbuf], ("hb", kbuf), KTB[:, :, kb * 128:(kb + 1) * 128], ("KT", kb))
                          dma("pool", VB[:, kb, :], cbv[su, kb * 128:(kb + 1) * 128, :], writes=[("VB", kb)])
                          if kb % 2 == 1:
                              jreverse_rows(VB[:, kb, :], ("VB", kb))
                  else:
                      qi = unit[1]
                      rows = 128
                      x1src = x1p[p, qi * 128:(qi + 1) * 128, :]
                  xa, xb = uiB % 2, 1 - uiB % 2
                  if uiB == 0:
                      dma("sp", xt[xa][:], x1src, writes=[("xt", xa)])
                  if uiB == 0:
                      rms_stats(xt[xa], ("xt", xa))
                  pre = uiB > 0
                  norm_to(xt[xa], ("xt", xa), gbc[0], "gbc0", hb[0][:], ("hb", 0), alt=pre)
                  norm_to(xt[xa], ("xt", xa), gbc[1], "gbc1", hb[1][:], ("hb", 1), alt=pre)
                  transpose_tile(hb[0], ("hb", 0), hkvT, "hkvT", "act")
                  transpose_tile(hb[1], ("hb", 1), hbT, "hbT", "dve")
                  for ncn in range(2):
                      b = gemm_tm(hkvT, "hkvT", 0, ncn * 512, WK[0])
                      evac(stage[0][:, ncn * 512:(ncn + 1) * 512], "stage0", b, eng="act")
                      S.add("dve", lambda e, ncn=ncn: e.tensor_copy(
                          out=hb[0][:, ncn * 512:(ncn + 1) * 512], in_=stage[0][:, ncn * 512:(ncn + 1) * 512]),
                          reads=["stage0"], writes=[("hb", 0)])
                  if qi % 2 == 0:
                      transpose_tile(hb[0], ("hb", 0), KTB[:, :, qi * 128:(qi + 1) * 128], ("KT", qi), "act")
                  else:
                      jtranspose_tile(hb[0], ("hb", 0), KTB[:, :, qi * 128:(qi + 1) * 128], ("KT", qi))
                  if is_s:
                      dma("sp", nbk_s[su], stage[0][0:64, :], reads=["stage0"])
                  else:
                      dma("sp", nbk_p[p, qi * 128:(qi + 1) * 128, :], stage[0][:], reads=["stage0"])
                  for ncn in range(2):
                      b = gemm_tm(hkvT, "hkvT", 0, D + ncn * 512, WK[1])
                      evac(xt[xb][:, ncn * 512:(ncn + 1) * 512], ("xt", xb), b, eng="act")
                      S.add("dve", lambda e, ncn=ncn, qi=qi, xb=xb: e.tensor_copy(
                          out=VB[:, qi, ncn * 512:(ncn + 1) * 512], in_=xt[xb][:, ncn * 512:(ncn + 1) * 512]),
                          reads=[("xt", xb)], writes=[("VB", qi)])
                  if is_s:
                      dma("sp", nbv_s[su], xt[xb][0:64, :], reads=[("xt", xb)])
                  else:
                      dma("sp", nbv_p[p, qi * 128:(qi + 1) * 128, :], xt[xb][:], reads=[("xt", xb)])
                  if uiB + 1 < len(unitsB):
                      nu = unitsB[uiB + 1]
                      nsrc = x1s[nu[1]] if nu[0] == "s" else x1p[p, nu[1] * 128:(nu[1] + 1) * 128, :]
                      dma("sp", xt[xb][:], nsrc, writes=[("xt", xb)])
                  if qi % 2 == 1:
                      jreverse_rows(VB[:, qi, :], ("VB", qi))
                  for ncn in range(2):
                      b = gemm_tm(hbT, "hbT", 0, 2 * D + ncn * 512, WK[2])
                      S.add("dve", lambda e, b=b, ncn=ncn: e.tensor_scalar(
                          out=hb[1][:, ncn * 512:(ncn + 1) * 512], in0=psg[b][:, :],
                          scalar1=-SCALE, scalar2=None, op0=ALU.mult),
                          reads=[("psg", b)], writes=[("hb", 1)])
                  transpose_tile(hb[1], ("hb", 1), nQT, "nQT", "dve")
                  for ncn in range(2):
                      b = gemm_tm(hbT, "hbT", 0, 3 * D + ncn * 512, WK[3])
                      evac(hb[1][:, ncn * 512:(ncn + 1) * 512], ("hb", 1), b, func=AF.Silu)

                  chk(8)
                  if uiB == len(unitsB) - 1 and p + 1 < NPASS:
                      load_w(0, w_in_a[:, :], 4 * D, WK[0:4])
                  def run_sb(qi, is_s):
                      steps = []
                      cbanks = [(psc[0], ("psc", 0)), (psc[1], ("psc", 1)),
                                (pst[0][:, :].bitcast(F32), ("pst", 0)), (pst[1][:, :].bitcast(F32), ("pst", 1))]
                      for kb in range(qi, -1, -1):
                          for ob in range(2):
                              for par in range(2):
                                  heads = [ob * 8 + 2 * a_ + par for a_ in range(4)]
                                  steps.append(dict(ob=ob, par=par, heads=heads, kb=kb, diag=(kb == qi),
                                                    last=(kb == 0), zb=len(steps) % 2, eb=len(steps) % 3, cb=cbanks[ob * 2 + par]))

                      cw = 64 if is_s else 128
                      nw = 4 * cw

                      def hv(t_, w_=None):
                          w_ = cw if w_ is None else w_
                          return t_[:, 0:4 * w_].rearrange("p (a c) -> p a c", a=4)[:, :, 0:cw]

                      def ekeys(sp_):
                          eb = sp_["eb"]
                          return F6[2 * eb], F6[2 * eb + 1], ("F6", 2 * eb), ("F6", 2 * eb + 1)

                      def stageA1(sp_):
                          zb, kb, heads, diag = sp_["zb"], sp_["kb"], sp_["heads"], sp_["diag"]
                          E_, SP_, ek, sk = ekeys(sp_)
                          for j in range(4):
                              h = heads[j]
                              fp = h // 2
                              r0 = (h % 2) * 64
                              S.add("pe", lambda e, zb=zb, j=j, fp=fp, r0=r0, kb=kb: e.matmul(
                                  pss[zb][:, j * cw:(j + 1) * cw], lhsT=KTB[r0:r0 + 64, fp, kb * 128:(kb + 1) * 128],
                                  rhs=nQT[r0:r0 + 64, fp, 0:cw], start=True, stop=True),
                                  reads=[("KT", kb), "nQT"], writes=[("pss", zb)])
                          S.add("act", lambda e, zb=zb, E_=E_: e.activation(out=E_[:, 0:nw], in_=pss[zb][:, 0:nw], func=AF.Exp, scale=-1.0),
                                reads=[("pss", zb)], writes=[ek])
                          S.add("act", lambda e, zb=zb, E_=E_, SP_=SP_: e.activation(out=SP_[:, 0:nw], in_=E_[:, 0:nw], func=AF.Ln, bias=1.0),
                                reads=[ek], writes=[sk])

                      def stageA2(sp_):
                          zb, kb, heads, diag = sp_["zb"], sp_["kb"], sp_["heads"], sp_["diag"]
                          cbt, cbk = sp_["cb"]
                          E_, SP_, ek, sk = ekeys(sp_)
                          rev = (kb % 2 == 1)
                          if diag:
                              mk, mkey = (m01R, "m01R") if rev else (m01, "m01")
                              S.add(POOLENG, lambda e, SP_=SP_, mk=mk: e.tensor_tensor(out=hv(SP_), in0=hv(SP_), in1=hv(mk, 128), op=ALU.mult),
                                    reads=[sk, mkey], writes=[sk])
                              S.add(POOLENG, lambda e, E_=E_, mk=mk: e.tensor_tensor(out=hv(E_), in0=hv(E_), in1=hv(mk, 128), op=ALU.mult),
                                    reads=[ek, mkey], writes=[ek])
                          else:
                              if rev:
                                  S.add("dve", lambda e, cbt=cbt, SP_=SP_: e.tensor_tensor(
                                      out=SP_[0:1, 0:nw], in0=cbt[0:1, 0:nw], in1=SP_[0:1, 0:nw], op=ALU.add),
                                      reads=[cbk, sk], writes=[sk])
                              else:
                                  S.add("dve", lambda e, cbt=cbt, SP_=SP_: e.scalar_tensor_tensor(
                                      out=SP_[96:128, 0:nw], in0=cbt[96:128, 0:nw], scalar=oh127[96:128, :],
                                      in1=SP_[96:128, 0:nw], op0=ALU.mult, op1=ALU.add),
                                      reads=[cbk, sk, "oh127"], writes=[sk])
                          S.add("dve", lambda e, zb=zb, SP_=SP_: e.tensor_copy(out=HI[zb][:, 0:nw], in_=SP_[:, 0:nw]),
                                reads=[sk], writes=[("HI", zb)])
                          S.add(POOLENG, lambda e, zb=zb, SP_=SP_: e.tensor_tensor(out=LO[zb][:, 0:nw], in0=SP_[:, 0:nw], in1=HI[zb][:, 0:nw], op=ALU.subtract),
                                reads=[sk, ("HI", zb)], writes=[("LO", zb)])

                      def stageB(sp_):
                          zb, kb, diag = sp_["zb"], sp_["kb"], sp_["diag"]
                          cbt, cbk = sp_["cb"]
                          E_, SP_, ek, sk = ekeys(sp_)
                          Lt, lkey = (LmR, "LmR") if (kb % 2 == 1) else (Lm, "Lm")
                          S.add("pe", lambda e, zb=zb, Lt=Lt, cbt=cbt: e.matmul(cbt[:, 0:nw], lhsT=Lt[:], rhs=HI[zb][:, 0:nw], start=True, stop=False),
                                reads=[lkey, ("HI", zb)], writes=[cbk])
                          S.add("pe", lambda e, zb=zb, Lt=Lt, cbt=cbt: e.matmul(cbt[:, 0:nw], lhsT=Lt[:], rhs=LO[zb][:, 0:nw], start=False, stop=True),
                                reads=[lkey, ("LO", zb)], writes=[cbk])
                          S.add("act", lambda e, cbt=cbt, SP_=SP_: e.activation(out=SP_[:, 0:nw], in_=cbt[:, 0:nw], func=AF.Exp, scale=-1.0),
                                reads=[cbk, ("HI", zb), ("LO", zb)], writes=[sk])

                      def stageB2(sp_):
                          zb = sp_["zb"]
                          E_, SP_, ek, sk = ekeys(sp_)
                          S.add("dve", lambda e, zb=zb, E_=E_, SP_=SP_: e.tensor_tensor(out=PB[zb][:, 0:nw], in0=E_[:, 0:nw], in1=SP_[:, 0:nw], op=ALU.mult),
                                reads=[ek, sk], writes=[("PB", zb)])

                      def stageC(sp_):
                          zb, kb, diag, heads, ob, par = sp_["zb"], sp_["kb"], sp_["diag"], sp_["heads"], sp_["ob"], sp_["par"]
                          for j in range(4):
                              h = heads[j]
                              oc = (h % 8) * 64
                              S.add("pe", lambda e, zb=zb, j=j, h=h, oc=oc, ob=ob, kb=kb, diag=diag, par=par: e.matmul(
                                  psg[ob][0:cw, oc:oc + 64], lhsT=PB[zb][:, j * cw:(j + 1) * cw],
                                  rhs=VB[:, kb, h * 64:(h + 1) * 64], start=(diag and j == 0 and par == 0), stop=(kb == 0),
                                  skip_group_check=True),
                                  reads=[("PB", zb), ("VB", kb)], writes=[("psg", ob)])
                          for _jk in range(JUNK):
                              S.add("pe", lambda e, ob=ob, kb=kb: e.matmul(
                                  psg[ob][0:cw, :], lhsT=Zm[:, 0:cw], rhs=VB[:, kb, ob * 512:(ob + 1) * 512],
                                  start=False, stop=(kb == 0), skip_group_check=True),
                                  reads=["Zm", ("VB", kb)], writes=[("psg", ob)])
                          if sp_["last"]:
                              def hview(t_, par=par):
                                  return t_.rearrange("p (a t e) -> p a t e", a=4, t=2)[:, :, par, :]
                              S.add("dve", lambda e, ob=ob, hview=hview: e.tensor_tensor(
                                  out=hview(hb[0][0:cw, ob * 512:(ob + 1) * 512]), in0=hview(psg[ob][0:cw, :]),
                                  in1=hview(hb[1][0:cw, ob * 512:(ob + 1) * 512]), op=ALU.mult),
                                  reads=[("psg", ob), ("hb", 1)], writes=[("hb", 0)])

                      nst = len(steps)
                      stageA1(steps[0])
                      if nst > 1:
                          stageA1(steps[1])
                      stageA2(steps[0])
                      for s_i in range(nst):
                          stageB(steps[s_i])
                          if s_i + 2 < nst:
                              stageA1(steps[s_i + 2])
                          if s_i + 1 < nst:
                              stageA2(steps[s_i + 1])
                          stageB2(steps[s_i])
                          if s_i >= 1:
                              stageC(steps[s_i - 1])
                      stageC(steps[nst - 1])
                  run_sb(qi, is_s)
                  if uiB + 1 < len(unitsB):
                      rms_stats(xt[xb], ("xt", xb), alt=True)
                  chk(9)
                  transpose_tile(hb[0], ("hb", 0), ogT[:, :, :], "ogT", "act")
                  for ncn in range(2):
                      b = gemm_tm(ogT, "ogT", 0, 4 * D + ncn * 512, WK[4])
                      S.add("dve", lambda e, b=b, ncn=ncn, xa=xa: e.tensor_tensor(
                          out=xt[xa][:, ncn * 512:(ncn + 1) * 512], in0=psg[b][:, :],
                          in1=xt[xa][:, ncn * 512:(ncn + 1) * 512], op=ALU.add),
                          reads=[("psg", b), ("xt", xa)], writes=[("xt", xa)])
                  rms_stats(xt[xa], ("xt", xa))
                  norm_to(xt[xa], ("xt", xa), gbc[2], "gbc2", stage[0][:], "stage0")
                  if is_s:
                      dma("sp", y_s[su], stage[0][0:64, :], reads=["stage0"])
                  else:
                      dma("sp", y_p[p, qi * 128:(qi + 1) * 128, :], stage[0][:], reads=["stage0"])
                  if uiB == len(unitsB) - 1 and p + 1 < NPASS:
                      load_w(4 * D, w_out_a[:, :], D, WK[4:5])

        try:
            body()
        except _Stop:
            S.barrier()
        S.emit(sems, dsems, block)
    return nc


def host_consts():
    ident = np.eye(128, dtype=np.float32)
    j = np.arange(128)[:, None]
    s = np.arange(128)[None, :]
    Lmat = (j >= s).astype(np.float32)
    m01 = (j < s).astype(np.float32)
    J = np.ascontiguousarray(ident[::-1])
    return np.concatenate([ident, Lmat, np.tile(m01, (1, 4)), J], axis=1)


def host_consts2():
    p = np.arange(128)[:, None]
    q = np.arange(128)[None, :]
    LR = (p <= q).astype(np.float32)
    mR = ((127 - p) < q).astype(np.float32)
    oh = np.zeros((128, 64), np.float32)
    oh[127, :] = 1.0
    return np.concatenate([LR, np.tile(mR, (1, 4)), oh], axis=1)


def host_ttab(rel_bias):
    ik = np.arange(128)[:, None]
    m = np.arange(640)[None, :]
    idx = np.clip(m - ik, -128, 128) + 128
    T = np.ascontiguousarray(rel_bias[:, idx]).astype(np.float32)
    T[:, 64:128, 0:64] = NEG
    T[:, 0:64, 576:640] = NEG
    return T


def make_in_maps(inp, SEQ, PAST, NPASS, ncores):
    NS = 2 * NPASS
    g128 = np.stack([np.broadcast_to(v, (128, D)) for v in
                     (inp["norm_a"][0], inp["norm_kv"], inp["norm_b"][0], inp["norm_f"])]).astype(np.float32)
    g128 = np.ascontiguousarray(g128)
    T = host_ttab(np.asarray(inp["rel_bias_a"][0]))
    cst = host_consts()
    cst2 = host_consts2()
    shared = {
        "w_in_a": np.ascontiguousarray(inp["w_in_a"][0]), "w_out_a": np.ascontiguousarray(inp["w_out_a"][0]),
        "w_kv": np.ascontiguousarray(inp["w_kv"]), "w_in_b": np.ascontiguousarray(inp["w_in_b"][0]),
        "w_out_b": np.ascontiguousarray(inp["w_out_b"][0]), "g128": g128, "ttab": T, "cst": cst, "cst2": cst2,
    }
    maps = []
    for c in range(ncores):
        m = dict(shared)
        m["xp"] = np.ascontiguousarray(inp["x_prompt"][c * NPASS:(c + 1) * NPASS])
        m["xs"] = np.ascontiguousarray(inp["x_sample"][c * NS:(c + 1) * NS])
        m["cak"] = np.ascontiguousarray(inp["cache_a_k"][0, c * NS:(c + 1) * NS]).reshape(NS, 512, D)
        m["cav"] = np.ascontiguousarray(inp["cache_a_v"][0, c * NS:(c + 1) * NS]).reshape(NS, 512, D)
        m["cbk"] = np.ascontiguousarray(inp["cache_b_k"][c * NS:(c + 1) * NS]).reshape(NS, PAST, D)
        m["cbv"] = np.ascontiguousarray(inp["cache_b_v"][c * NS:(c + 1) * NS]).reshape(NS, PAST, D)
        maps.append(m)
    return maps


def assemble(results, SEQ, NPASS):
    def cat(name):
        return np.concatenate([r[name] for r in results], axis=0)
    B = len(results) * NPASS
    NSs = 2 * B
    y_p = cat("y_p")
    y_s = cat("y_s")
    nak_p = cat("nak_p").reshape(1, B, 512, H, DH)
    nav_p = cat("nav_p").reshape(1, B, 512, H, DH)
    nbk_p = cat("nbk_p").reshape(B, SEQ, H, DH)
    nbv_p = cat("nbv_p").reshape(B, SEQ, H, DH)
    nak_s = cat("nak_s").reshape(1, NSs, 64, H, DH)
    nav_s = cat("nav_s").reshape(1, NSs, 64, H, DH)
    nbk_s = cat("nbk_s").reshape(NSs, 64, H, DH)
    nbv_s = cat("nbv_s").reshape(NSs, 64, H, DH)
    return (y_p, y_s, nak_p, nav_p, nbk_p, nbv_p, nak_s, nav_s, nbk_s, nbv_s)


def kernel(**inputs):
    inp = {k: np.asarray(v) for k, v in inputs.items()}
    SEQ = inp["x_prompt"].shape[1]
    PAST = inp["cache_b_k"].shape[1]
    NPASS = inp["x_prompt"].shape[0] // NCORES
    nc = build_program(SEQ, PAST, NPASS)
    maps = make_in_maps(inp, SEQ, PAST, NPASS, NCORES)
    res = run_bass_kernel_spmd(nc, maps, core_ids=list(range(NCORES)))
    outs = assemble(res.results, SEQ, NPASS)
    return tuple(np.ascontiguousarray(o, dtype=np.float32) for o in outs)
```

```python
import numpy as np
from contextlib import ExitStack
import concourse.bass as bass
import concourse.mybir as mybir
from concourse.bass_utils import run_bass_kernel_spmd

F32 = mybir.dt.float32
BF16 = mybir.dt.bfloat16
AF = mybir.ActivationFunctionType
ALU = mybir.AluOpType

D = 1024
H = 16
DH = 64
KC = 8
SCALE = DH ** -0.5
NEG = -30000.0
EPS = 1e-6
NCORES = 8


class Ins:
    __slots__ = ("eng", "fn", "idx", "waits", "signal", "dma", "sem", "val", "count")

    def __init__(self, eng, fn, idx, dma):
        self.eng = eng; self.fn = fn; self.idx = idx; self.dma = dma
        self.waits = []; self.signal = False; self.sem = None; self.val = None; self.count = None


class Sched:
    ENGS = ("pe", "act", "dve", "pool", "sp")
    NDMA = 16

    def __init__(self):
        self.streams = {e: [] for e in self.ENGS}
        self.last_w = {}
        self.readers = {}
        self.waited = {e: {f: -1 for f in self.ENGS} for e in self.ENGS}
        self.waited_dma = {e: set() for e in self.ENGS}
        self.dma_list = {"sp": [], "pool": []}
        self.live_dma = []

    def _resolve(self, ins, deps):
        eng = ins.eng
        for p in deps:
            if p is ins or p is None:
                continue
            if p.dma:
                if id(p) in self.waited_dma[eng]:
                    continue
                self.waited_dma[eng].add(id(p))
                ins.waits.append(p)
            else:
                if p.eng == eng and eng == "pe":
                    continue
                if self.waited[eng][p.eng] >= p.idx:
                    continue
                self.waited[eng][p.eng] = p.idx
                p.signal = True
                ins.waits.append(p)

    def add(self, eng, fn, reads=(), writes=(), dma=False):
        st = self.streams[eng]
        ins = Ins(eng, fn, len(st), dma)
        ps_reads = [r for r in reads if isinstance(r, tuple) and r[0] in ("psg", "pst", "pss", "psc")]
        if ps_reads:
            writes = list(writes) + [r for r in ps_reads if r not in writes]
        deps = []
        for r in reads:
            lw = self.last_w.get(r)
            if lw is not None:
                deps.append(lw)
        for w in writes:
            lw = self.last_w.get(w)
            if lw is not None:
                deps.append(lw)
            deps.extend(self.readers.get(w, ()))
        if dma:
            q = self.dma_list[eng]
            j = len(q)
            ins.sem = (eng, j % self.NDMA)
            ins.val = 16 * (j // self.NDMA + 1)
            if j >= self.NDMA:
                deps.append(q[j - self.NDMA])
            q.append(ins)
            self.live_dma.append(ins)
        self._resolve(ins, deps)
        for r in reads:
            lst = self.readers.setdefault(r, [])
            if not dma:
                lst[:] = [x for x in lst if x.dma or x.eng != eng]
            lst.append(ins)
        for w in writes:
            self.last_w[w] = ins
            self.readers[w] = []
        st.append(ins)
        return ins

    def barrier(self):
        lasts = {}
        for e in self.ENGS:
            real = [x for x in self.streams[e] if x.fn is not None and not x.dma]
            lasts[e] = real[-1] if real else None
        dmas = list(self.live_dma)
        for e in self.ENGS:
            ins = Ins(e, None, len(self.streams[e]), False)
            deps = [lasts[f] for f in self.ENGS if f != e and lasts[f] is not None] + dmas
            self._resolve(ins, deps)
            self.streams[e].append(ins)
        self.live_dma = []
        self.last_w = {}
        self.readers = {}

    def emit(self, sems, dsems, block):
        for e in self.ENGS:
            c = 0
            for ins in self.streams[e]:
                if ins.dma or ins.fn is None:
                    continue
                if ins.signal:
                    c += 1
                    ins.count = c

        def run(engname, eng):
            for ins in self.streams[engname]:
                best = {}
                for p in ins.waits:
                    if p.dma:
                        key = ("d",) + p.sem
                        v = p.val
                        s = dsems[p.sem[0]][p.sem[1]]
                    else:
                        key = ("e", p.eng)
                        v = p.count
                        s = sems[p.eng]
                    if key not in best or best[key][1] < v:
                        best[key] = (s, v)
                for s, v in best.values():
                    eng.wait_ge(s, v)
                if ins.fn is None:
                    continue
                bi = ins.fn(eng)
                if ins.dma:
                    bi.then_inc(dsems[ins.sem[0]][ins.sem[1]], 16)
                elif ins.signal:
                    bi.then_inc(sems[engname], 1)
            if engname in self.dma_list:
                last = {}
                for ins in self.dma_list[engname]:
                    last[ins.sem] = ins.val
                for sm, v in last.items():
                    eng.wait_ge(dsems[sm[0]][sm[1]], v)

        @block.tensor
        def _(eng):
            run("pe", eng)

        @block.scalar
        def _(eng):
            run("act", eng)

        @block.vector
        def _(eng):
            run("dve", eng)

        @block.gpsimd
        def _(eng):
            run("pool", eng)

        @block.sync
        def _(eng):
            run("sp", eng)


def build_program(SEQ, PAST, NPASS):
    NT = SEQ // 128
    NST = SEQ // 512
    NKB = PAST // 128
    NKBT = max(NT, NKB + 1)
    KTN = NKBT * 128
    NS = 2 * NPASS
    nc = bass.Bass("TRN2", target_bir_lowering=False)

    def din(name, shape):
        return nc.dram_tensor(name, shape, F32, kind="ExternalInput").ap()

    def dout(name, shape):
        return nc.dram_tensor(name, shape, F32, kind="ExternalOutput").ap()

    xp = din("xp", [NPASS, SEQ, D])
    xs = din("xs", [NS, 64, D])
    cak = din("cak", [NS, 512, D])
    cav = din("cav", [NS, 512, D])
    cbk = din("cbk", [NS, PAST, D])
    cbv = din("cbv", [NS, PAST, D])
    w_in_a = din("w_in_a", [D, 4 * D])
    w_out_a = din("w_out_a", [D, D])
    w_kv = din("w_kv", [D, 2 * D])
    w_in_b = din("w_in_b", [D, 2 * D])
    w_out_b = din("w_out_b", [D, D])
    g128 = din("g128", [4, 128, D])
    ttab = din("ttab", [H, 128, 640])
    cst = din("cst", [128, 896])
    cst2 = din("cst2", [128, 704])
    y_p = dout("y_p", [NPASS, SEQ, D])
    y_s = dout("y_s", [NS, 64, D])
    nak_p = dout("nak_p", [NPASS, 512, D])
    nav_p = dout("nav_p", [NPASS, 512, D])
    nbk_p = dout("nbk_p", [NPASS, SEQ, D])
    nbv_p = dout("nbv_p", [NPASS, SEQ, D])
    nak_s = dout("nak_s", [NS, 64, D])
    nav_s = dout("nav_s", [NS, 64, D])
    nbk_s = dout("nbk_s", [NS, 64, D])
    nbv_s = dout("nbv_s", [NS, 64, D])
    x1p = nc.dram_tensor("x1p", [NPASS, SEQ, D], F32).ap()
    x1s = nc.dram_tensor("x1s", [NS, 128, D], F32).ap()

    S = Sched()
    with ExitStack() as es:
        def sb(name, shape, dt):
            return es.enter_context(nc.sbuf_tensor(name, shape, dt))

        def ps(name, shape, dt):
            return es.enter_context(nc.psum_tensor(name, shape, dt))

        W = sb("W", [128, KC, 5 * D], BF16)
        KTF = sb("KTF", [128, max(KC * KTN, 16384)], BF16)
        VVB = sb("VVB", [128, max(NKBT * D, 16512)], BF16)
        xt = [sb(f"xt{i}", [128, D], F32) for i in range(2)]
        hb = [sb(f"hb{i}", [128, D], BF16) for i in range(2)]
        hTf = sb("hTf", [128, KC * 512], BF16)
        QTf = sb("QTf", [128, KC * 512], BF16)
        hkT = sb("hkT", [128, KC, 256], BF16)
        ogT = sb("ogT", [128, KC, 128], BF16)
        TTb = sb("TTb", [128, 1280], F32)
        F01 = sb("F01", [128, 1024], F32)
        gbc0 = sb("gbc0", [128, D], F32)
        stage = [sb(f"stage{i}", [128, D], F32) for i in range(2)]
        ident = sb("ident", [128, 128], BF16)
        Lm = sb("Lm", [128, 128], BF16)
        m01 = sb("m01", [128, 512], BF16)
        Jm = sb("Jm", [128, 128], BF16)
        Zm = sb("Zm", [128, 128], BF16)
        LmR = sb("LmR", [128, 128], BF16)
        m01R = sb("m01R", [128, 512], BF16)
        small = sb("small", [128, 16], F32)
        KTB = KTF[:, 0:KC * KTN].rearrange("p (k n) -> p k n", k=KC)
        KTAv = KTF[:, 0:KC * 1024].rearrange("p (k n) -> p k n", k=KC)
        ogA = KTF[:, 8192:12288].rearrange("p (m n) -> p m n", m=4)
        sgA = KTF[:, 12288:16384].rearrange("p (m n) -> p m n", m=4)
        hT = hTf[:, :].rearrange("p (k n) -> p k n", k=KC)
        QT = QTf[:, :].rearrange("p (k n) -> p k n", k=KC)
        PTA = [VVB[:, 8320 + i * 4096: 8320 + (i + 1) * 4096].rearrange("p (k n) -> p k n", k=8) for i in range(2)]
        PBt = hTf[:, 0:3072].rearrange("p (k n) -> p k n", k=6)
        nQT = hTf[:, 3072:4096].rearrange("p (k n) -> p k n", k=KC)
        gbc = [gbc0, QTf[:, 0:2048].bitcast(F32), QTf[:, 2048:4096].bitcast(F32)]
        TT = [TTb[:, 0:640], TTb[:, 640:1280]]
        F = [F01[:, 0:512], F01[:, 512:1024], TTb[:, 0:512], TTb[:, 512:1024]]
        F6 = [F01[:, 0:512], F01[:, 512:1024], TTb[:, 0:512], TTb[:, 512:1024], stage[1][:, 0:512], stage[1][:, 512:1024]]
        carryT = hkT[0:1, :, :].rearrange("p k n -> p (k n)")
        ktok = hb[0]
        cstf = stage[0]
        psg = [ps(f"psg{i}", [128, 512], F32) for i in range(2)]
        pst = [ps(f"pst{i}", [128, 1024], BF16) for i in range(2)]
        pss = [ps(f"pss{i}", [128, 512], F32) for i in range(2)]
        psc = [ps(f"psc{i}", [128, 512], F32) for i in range(2)]

        sems = {e: es.enter_context(nc.semaphore("s_" + e)) for e in Sched.ENGS}
        dsems = {q: [es.enter_context(nc.semaphore(f"d_{q}{i}")) for i in range(Sched.NDMA)]
                 for q in ("sp", "pool")}
        block = es.enter_context(nc.Block())

        KTA = KTAv
        VA = VVB[:, 0:8 * H * 65].rearrange("p (k h e) -> p k h e", k=8, h=H)
        VB = VVB[:, 0:NKBT * D].rearrange("p (k n) -> p k n", k=NKBT)
        HI = [PBt[:, 0 + i, :] for i in range(2)]
        LO = [PBt[:, 2 + i, :] for i in range(2)]
        PB = [PBt[:, 4 + i, :] for i in range(2)]
        ss = small[:, 0:1]
        lnv = small[:, 1:2]
        rstd = small[:, 2:3]
        rc = small[:, 4:8]
        epsb = small[:, 8:9]
        oh127 = small[:, 9:10]

        def dma(q, out, in_, reads=(), writes=()):
            S.add(q, lambda e: e.dma_start(out=out, in_=in_), reads=reads, writes=writes, dma=True)

        def load_w(dst_col0, src, ncols, keys):
            srcv = src.rearrange("(kc p) n -> p kc n", p=128)
            for c in range(0, ncols, 512):
                key = keys[(c // 1024)]
                dma("pool", W[:, :, dst_col0 + c:dst_col0 + c + 512], srcv[:, :, c:c + 512],
                    writes=[key])

        def rms_stats(xtile, xkey, alt=False):
            ss_, ln_, rs_ = (small[:, 10:11], small[:, 11:12], small[:, 12:13]) if alt else (ss, lnv, rstd)
            tg = "2" if alt else ""
            S.add("act", lambda e: e.activation(out=ogT[:, :, :].rearrange("p k n -> p (k n)"), in_=xtile[:],
                                                func=AF.Square, accum_out=ss_),
                  reads=[xkey], writes=["ogT", "ss" + tg])
            S.add("act", lambda e: e.activation(out=ln_, in_=ss_, func=AF.Ln, scale=1.0 / D, bias=epsb),
                  reads=["ss" + tg, "epsb"], writes=["lnv" + tg])
            S.add("act", lambda e: e.activation(out=rs_, in_=ln_, func=AF.Exp, scale=-0.5),
                  reads=["lnv" + tg], writes=["rstd" + tg])

        def norm_to(xtile, xkey, g, gkey, dst, dkey, alt=False):
            rs_ = small[:, 12:13] if alt else rstd
            rk = "rstd2" if alt else "rstd"
            S.add("dve", lambda e: e.scalar_tensor_tensor(out=dst, in0=xtile[:], scalar=rs_, in1=g[:],
                                                          op0=ALU.mult, op1=ALU.mult),
                  reads=[xkey, rk, gkey], writes=[dkey])

        tcount = [0]

        def transpose_tile(src, skey, dst, dkey, evac_eng):
            b = tcount[0] % 2
            tcount[0] += 1
            for k in range(KC):
                S.add("pe", lambda e, k=k: e.transpose(out=pst[b][:, k * 128:(k + 1) * 128],
                                                       in_=src[:, k * 128:(k + 1) * 128], identity=ident[:]),
                      reads=[skey, "ident"], writes=[("pst", b)])
            pv = pst[b][:, :].rearrange("p (k n) -> p k n", k=KC)
            if evac_eng == "act":
                S.add("act", lambda e: e.copy(out=dst, in_=pv), reads=[("pst", b)], writes=[dkey])
            else:
                S.add("dve", lambda e: e.tensor_copy(out=dst, in_=pv), reads=[("pst", b)], writes=[dkey])

        gcount = [0]

        def jtranspose_tile(src, skey, dst, dkey):
            for half in range(2):
                b = gcount[0] % 2
                gcount[0] += 1
                for k4 in range(4):
                    k = half * 4 + k4
                    S.add("pe", lambda e, b=b, k=k, k4=k4: e.matmul(psg[b][:, k4 * 128:(k4 + 1) * 128],
                                                                  lhsT=src[:, k * 128:(k + 1) * 128], rhs=Jm[:],
                                                                  start=True, stop=True),
                          reads=[skey, "Jm"], writes=[("psg", b)])
                pv = psg[b][:, :].rearrange("p (k n) -> p k n", k=4)
                if half == 0:
                    S.add("act", lambda e, pv=pv, half=half: e.copy(out=dst[:, half * 4:(half + 1) * 4, :], in_=pv),
                          reads=[("psg", b)], writes=[dkey])
                else:
                    S.add("dve", lambda e, pv=pv, half=half: e.tensor_copy(out=dst[:, half * 4:(half + 1) * 4, :], in_=pv),
                          reads=[("psg", b)], writes=[dkey])

        def jreverse_rows(vt, vkey):
            bs = []
            for ncn in range(2):
                b = gcount[0] % 2
                gcount[0] += 1
                bs.append(b)
                S.add("pe", lambda e, b=b, ncn=ncn: e.matmul(psg[b][:, :], lhsT=Jm[:], rhs=vt[:, ncn * 512:(ncn + 1) * 512],
                                                          start=True, stop=True),
                      reads=[vkey, "Jm"], writes=[("psg", b)])
            for ncn in range(2):
                b = bs[ncn]
                eng = "act" if ncn == 0 else "dve"
                if eng == "act":
                    S.add("act", lambda e, b=b, ncn=ncn: e.copy(out=vt[:, ncn * 512:(ncn + 1) * 512], in_=psg[b][:, :]),
                          reads=[("psg", b)], writes=[vkey])
                else:
                    S.add("dve", lambda e, b=b, ncn=ncn: e.tensor_copy(out=vt[:, ncn * 512:(ncn + 1) * 512], in_=psg[b][:, :]),
                          reads=[("psg", b)], writes=[vkey])

        def gemm_fm(wcol, wkey, rhsT, rkey, n, ncol_groups=1):
            b = gcount[0] % 2
            gcount[0] += 1
            for g in range(ncol_groups):
                for k in range(KC):
                    S.add("pe", lambda e, k=k, g=g: e.matmul(psg[b][:, g * n:(g + 1) * n],
                                                             lhsT=W[:, k, wcol + g * 128:wcol + (g + 1) * 128],
                                                             rhs=rhsT[:, k, 0:n], start=(k == 0), stop=(k == KC - 1)),
                          reads=[wkey, rkey], writes=[("psg", b)])
            return b

        def gemm_tm(lhsT_t, lkey, tok0, wcol, wkey):
            b = gcount[0] % 2
            gcount[0] += 1
            for k in range(KC):
                S.add("pe", lambda e, k=k: e.matmul(psg[b][:, :], lhsT=lhsT_t[:, k, tok0:tok0 + 128],
                                                    rhs=W[:, k, wcol:wcol + 512], start=(k == 0), stop=(k == KC - 1)),
                      reads=[wkey, lkey], writes=[("psg", b)])
            return b

        ecount = [0]

        def evac(out, okey, b, n=512, scale=None, eng=None, func=None, extra_reads=(), groups=None):
            if eng is None:
                eng = "act" if ecount[0] % 2 == 0 else "dve"
                ecount[0] += 1
            src = psg[b][:, 0:n]
            if groups is not None:
                src = psg[b][:, 0:n].rearrange("p (g n) -> p g n", g=groups)
            okeys = okey if isinstance(okey, list) else [okey]
            if func is not None:
                S.add("act", lambda e: e.activation(out=out, in_=src, func=func),
                      reads=[("psg", b)] + list(extra_reads), writes=okeys)
            elif eng == "act":
                if scale is None:
                    S.add("act", lambda e: e.copy(out=out, in_=src), reads=[("psg", b)], writes=okeys)
                else:
                    S.add("act", lambda e: e.activation(out=out, in_=src, func=AF.Copy, scale=scale),
                          reads=[("psg", b)], writes=okeys)
            else:
                if scale is None:
                    S.add("dve", lambda e: e.tensor_copy(out=out, in_=src), reads=[("psg", b)], writes=okeys)
                else:
                    S.add("dve", lambda e: e.tensor_scalar(out=out, in0=src, scalar1=scale, scalar2=None,
                                                           op0=ALU.mult),
                          reads=[("psg", b)], writes=okeys)

        scount = [0]

        def stage_out(dst_dram, b0, rows=128):
            pass

        dma("sp", cstf[:, 0:896], cst[:, :], writes=["stage0"])
        dma("sp", stage[1][:, 0:704], cst2[:, :], writes=["stage1"])
        S.add("dve", lambda e: e.tensor_copy(out=ident[:], in_=cstf[:, 0:128]), reads=["stage0"], writes=["ident"])
        S.add("dve", lambda e: e.tensor_copy(out=Lm[:], in_=cstf[:, 128:256]), reads=["stage0"], writes=["Lm"])
        S.add("dve", lambda e: e.tensor_copy(out=m01[:], in_=cstf[:, 256:768]), reads=["stage0"], writes=["m01"])
        S.add("dve", lambda e: e.tensor_copy(out=Jm[:], in_=cstf[:, 768:896]), reads=["stage0"], writes=["Jm"])
        S.add("dve", lambda e: e.tensor_copy(out=LmR[:], in_=stage[1][:, 0:128]), reads=["stage1"], writes=["LmR"])
        S.add("dve", lambda e: e.tensor_copy(out=m01R[:], in_=stage[1][:, 128:640]), reads=["stage1"], writes=["m01R"])
        S.add("dve", lambda e: e.tensor_copy(out=oh127, in_=stage[1][:, 640:641]), reads=["stage1"], writes=["oh127"])
        S.add("dve", lambda e: e.memset(epsb, EPS), writes=["epsb"])
        S.add("dve", lambda e: e.memset(Zm[:], 0.0), writes=["Zm"])

        WK = [("W", i) for i in range(5)]

        KSTOP = 0
        POOLENG = "dve"
        JUNK = 0

        class _Stop(Exception):
            pass

        ui_box = [0]

        def chk(n_):
            if KSTOP == ui_box[0] * 100 + n_:
                raise _Stop()

        def body():
          for p in range(NPASS):
              S.barrier()
              if p == 0:
                  load_w(0, w_in_a[:, :], 4 * D, WK[0:4])
                  load_w(4 * D, w_out_a[:, :], D, WK[4:5])
              dma("sp", gbc[0][:], g128[0], writes=["gbc0"])
              S.add("pool", lambda e: e.memset(KTF[:, 0:KC * 1024], 0.0), writes=[("KTA", s) for s in range(8)])
              S.add("pool", lambda e: e.memset(VVB[:, 0:8 * H * 65], 0.0), writes=[("VA", s) for s in range(8)])
              S.add("pool", lambda e: e.memset(VA[:, :, :, 64:65], 1.0), writes=[("VA", s) for s in range(8)])
              for i in range(2):
                  S.add("pool", lambda e, i=i: e.memset(VVB[:, 8320 + i * 4096: 8320 + (i + 1) * 4096], 0.0),
                        writes=[("PT", i)])
              S.add("pool", lambda e: e.memset(KTF[:, 8192:16384], 0.0), writes=["og", "sg"])
              chk(1)

              units = [("p", st) for st in range(NST)] + [("s", 2 * p), ("s", 2 * p + 1)]
              for ui, unit in enumerate(units):
                  ui_box[0] = ui
                  is_s = unit[0] == "s"
                  ntt = 1 if is_s else 4
                  n = ntt * 128
                  if is_s:
                      su = unit[1]
                      curh, prevh = 1, 0
                      kbs_avail = [0, 1, 2, 3, 4]
                      qchunks = [8]
                      want_out = True
                  else:
                      st = unit[1]
                      curh, prevh = st % 2, 1 - st % 2
                      kbs_avail = list(range(4, 8)) if st == 0 else list(range(8))
                      qchunks = list(range(8, 16))
                      want_out = (st == NST - 1)

                  def slot(kb):
                      return (prevh * 4 + kb) if kb < 4 else (curh * 4 + kb - 4)

                  def xsrc(tt):
                      if is_s:
                          return xs[su], 64
                      return xp[p, (st * 4 + tt) * 128:(st * 4 + tt + 1) * 128, :], 128

                  if is_s:
                      for kb in range(4):
                          dma("pool", ktok[:], cak[su, kb * 128:(kb + 1) * 128, :], writes=[("hb", 0)])
                          dstv = KTA[:, :, slot(kb) * 128:(slot(kb) + 1) * 128]
                          transpose_tile(ktok, ("hb", 0), dstv, ("KTA", slot(kb)), "act")
                          dma("pool", VA[:, slot(kb), :, 0:64],
                              cav[su, kb * 128:(kb + 1) * 128, :].rearrange("t (h e) -> t h e", h=H),
                              writes=[("VA", slot(kb))])

                  def prologue_tile(unit_, tt_):
                      if unit_[0] == "s":
                          src_, rows_ = xs[unit_[1]], 64
                      else:
                          src_, rows_ = xp[p, (unit_[1] * 4 + tt_) * 128:(unit_[1] * 4 + tt_ + 1) * 128, :], 128
                      xb_ = tt_ % 2
                      if rows_ < 128:
                          S.add("pool", lambda e, xb_=xb_: e.memset(xt[xb_][64:128, :], 0.0), writes=[("xt", xb_)])
                      dma("sp", xt[xb_][0:rows_, :], src_, writes=[("xt", xb_)])
                      rms_stats(xt[xb_], ("xt", xb_))
                      norm_to(xt[xb_], ("xt", xb_), gbc[0], "gbc0", hb[xb_][:], ("hb", xb_))
                      transpose_tile(hb[xb_], ("hb", xb_), hT[:, :, tt_ * 128:(tt_ + 1) * 128], "hT", "act")

                  if ui == 0:
                      for tt in range(ntt):
                          prologue_tile(unit, tt)
                  nxt = units[ui + 1] if ui + 1 < len(units) else None
                  nxt_ntt = 0 if nxt is None else (1 if nxt[0] == "s" else 4)
                  chk(2)

                  for fp in range(8):
                      b = gemm_fm(fp * 128, WK[0], hT, "hT", n)
                      evac(QT[:, fp, 0:n], "QT", b, n)
                  for fp in range(8):
                      b = gemm_fm(D + fp * 128, WK[1], hT, "hT", n)
                      evac(KTA[:, fp, curh * 512:curh * 512 + n], [("KTA", curh * 4 + q_) for q_ in range(4)], b, n)
                  chk(3)

                  for tt in range(ntt):
                      sl = curh * 4 + tt
                      for ncn in range(2):
                          b = gemm_tm(hT, "hT", tt * 128, 2 * D + ncn * 512, WK[2])
                          S.add("dve", lambda e, b=b, sl=sl, ncn=ncn: e.tensor_copy(
                              out=VA[:, sl, ncn * 8:(ncn + 1) * 8, 0:64],
                              in_=psg[b][:, :].rearrange("p (h e) -> p h e", h=8)),
                              reads=[("psg", b)], writes=[("VA", sl)])
                          if want_out:
                              S.add("act", lambda e, b=b, ncn=ncn: e.copy(out=stage[0][:, ncn * 512:(ncn + 1) * 512],
                                                                       in_=psg[b][:, :]),
                                    reads=[("psg", b)], writes=["stage0"])
                      if want_out:
                          if is_s:
                              dma("sp", nav_s[su], stage[0][0:64, :], reads=["stage0"])
                          else:
                              dma("sp", nav_p[p, tt * 128:(tt + 1) * 128, :], stage[0][:], reads=["stage0"])
                          for ncn in range(2):
                              b = gemm_tm(hT, "hT", tt * 128, D + ncn * 512, WK[1])
                              evac(stage[0][:, ncn * 512:(ncn + 1) * 512], "stage0", b, eng="act")
                          if is_s:
                              dma("sp", nak_s[su], stage[0][0:64, :], reads=["stage0"])
                          else:
                              dma("sp", nak_p[p, tt * 128:(tt + 1) * 128, :], stage[0][:], reads=["stage0"])
                      for ncn in range(2):
                          b = gemm_tm(hT, "hT", tt * 128, 3 * D + ncn * 512, WK[3])
                          evac(sgA[:, tt, ncn * 512:(ncn + 1) * 512], "sg", b, func=AF.Silu)
                  chk(4)
                  if ui == len(units) - 1:
                      load_w(0, w_kv[:, :], 2 * D, WK[0:2])
                      load_w(2 * D, w_in_b[:, :], 2 * D, WK[2:4])

                  npair = 1 if is_s else 4
                  def st1(h):
                      tb = h % 2
                      fp = h // 2
                      r0 = (h % 2) * 64
                      dma("sp", TT[tb][:], ttab[h], writes=[("TT", tb)])
                      pb = h % 2
                      for kb in kbs_avail:
                          cs = [c for c in qchunks if 2 * kb <= c <= 2 * kb + 9]
                          if not cs:
                              continue
                          c0, c1 = cs[0], cs[-1]
                          ncol = (c1 - c0 + 1) * 64
                          q0 = (c0 - 8) * 64
                          d0 = c0 - 2 * kb
                          sl = slot(kb)
                          sbk = scount[0] % 2
                          scount[0] += 1
                          S.add("pe", lambda e, sbk=sbk, sl=sl, ncol=ncol, q0=q0, fp=fp, r0=r0: e.matmul(
                              pss[sbk][:, 0:ncol], lhsT=KTA[r0:r0 + 64, fp, sl * 128:(sl + 1) * 128],
                              rhs=QT[r0:r0 + 64, fp, q0:q0 + ncol], start=True, stop=True),
                              reads=[("KTA", sl), "QT"], writes=[("pss", sbk)])
                          S.add("dve", lambda e, sbk=sbk, ncol=ncol, d0=d0, tb=tb: e.scalar_tensor_tensor(
                              out=F[sbk][:, 0:ncol], in0=pss[sbk][:, 0:ncol], scalar=SCALE,
                              in1=TT[tb][:, d0 * 64:d0 * 64 + ncol], op0=ALU.mult, op1=ALU.add),
                              reads=[("pss", sbk), ("TT", tb)], writes=[("F", sbk)])
                          S.add("act", lambda e, sbk=sbk, ncol=ncol, q0=q0, kb=kb, pb=pb: e.activation(
                              out=PTA[pb][:, kb, q0:q0 + ncol], in_=F[sbk][:, 0:ncol], func=AF.Exp),
                              reads=[("F", sbk)], writes=[("PT", pb)])

                  def st2(h):
                      pb = h % 2
                      ob = h % 2
                      for m in range(npair):
                          kl = [kb for kb in range(m, m + 5) if kb in kbs_avail]
                          for j, kb in enumerate(kl):
                              sl = slot(kb)
                              if is_s:
                                  S.add("pe", lambda e, ob=ob, kb=kb, sl=sl, h=h, j=j, kl=kl, pb=pb: e.matmul(
                                      psc[ob][0:64, 0:65], lhsT=PTA[pb][:, kb, 0:64], rhs=VA[:, sl, h, :],
                                      start=(j == 0), stop=(j == len(kl) - 1)),
                                      reads=[("PT", pb), ("VA", sl)], writes=[("psc", ob)])
                              else:
                                  S.add("pe", lambda e, ob=ob, kb=kb, sl=sl, h=h, j=j, kl=kl, m=m, pb=pb: e.matmul(
                                      psc[ob][:, m * 65:(m + 1) * 65], lhsT=PTA[pb][:, kb, m * 128:(m + 1) * 128],
                                      rhs=VA[:, sl, h, :], start=(j == 0), stop=(j == len(kl) - 1)),
                                      reads=[("PT", pb), ("VA", sl)], writes=[("psc", ob)])
                      rows = 64 if is_s else 128
                      pv = psc[ob][0:rows, 0:npair * 65].rearrange("p (m e) -> p m e", m=npair)
                      S.add("dve", lambda e, pv=pv, rows=rows, npair=npair: e.reciprocal(out=rc[0:rows, 0:npair], in_=pv[:, :, 64]),
                            reads=[("psc", ob)], writes=["rc"])
                      for m in range(npair):
                          S.add("dve", lambda e, pv=pv, m=m, h=h, rows=rows: e.scalar_tensor_tensor(
                              out=ogA[0:rows, m, h * 64:(h + 1) * 64], in0=pv[:, m, 0:64], scalar=rc[0:rows, m:m + 1],
                              in1=sgA[0:rows, m, h * 64:(h + 1) * 64], op0=ALU.mult, op1=ALU.mult),
                              reads=[("psc", ob), "rc", "sg"], writes=["og"])

                  st1(0)
                  for h in range(H):
                      if h + 1 < H:
                          st1(h + 1)
                      st2(h)
                      if h % 4 == 3 and h // 4 < nxt_ntt:
                          prologue_tile(nxt, h // 4)
                  if is_s:
                      S.add("dve", lambda e: e.memset(ogA[64:128, 0, :], 0.0), writes=["og"])
                  chk(5)

                  for tt in range(ntt):
                      transpose_tile(ogA[:, tt, :], "og", ogT[:, :, :], "ogT", "act")
                      if is_s:
                          chk(51)
                      xb = tt % 2
                      src, rows = xsrc(tt)
                      if rows < 128:
                          S.add("pool", lambda e, xb=xb: e.memset(xt[xb][64:128, :], 0.0), writes=[("xt", xb)])
                      dma("sp", xt[xb][0:rows, :], src, writes=[("xt", xb)])
                      for ncn in range(2):
                          b = gemm_tm(ogT, "ogT", 0, 4 * D + ncn * 512, WK[4])
                          S.add("dve", lambda e, b=b, ncn=ncn, xb=xb: e.tensor_tensor(
                              out=stage[1][:, ncn * 512:(ncn + 1) * 512], in0=psg[b][:, :],
                              in1=xt[xb][:, ncn * 512:(ncn + 1) * 512], op=ALU.add),
                              reads=[("psg", b), ("xt", xb)], writes=["stage1"])
                      if is_s:
                          chk(52)
                      if is_s:
                          dma("sp", x1s[su], stage[1][:], reads=["stage1"])
                          chk(53)
                      else:
                          dma("sp", x1p[p, (st * 4 + tt) * 128:(st * 4 + tt + 1) * 128, :], stage[1][:],
                              reads=["stage1"])
                  chk(60)
                  if ui == len(units) - 1:
                      load_w(4 * D, w_out_b[:, :], D, WK[4:5])

              ui_box[0] = 9
              chk(6)
              S.barrier()
              for i in range(3):
                  dma("sp", gbc[i][:], g128[1 + i], writes=[f"gbc{i}"])
              chk(7)

              hkvT = hkT[:, :, 0:128]
              hbT = hkT[:, :, 128:256]
              unitsB = [("p", i) for i in range(NT)] + [("s", 2 * p), ("s", 2 * p + 1)]
              ccount = [0]
              for uiB, unit in enumerate(unitsB):
                  ui_box[0] = 10 + uiB
                  is_s = unit[0] == "s"
                  if is_s:
                      su = unit[1]
                      qi = NKB
                      rows = 64
                      x1src = x1s[su]
                      for kb in range(NKB):
                          kbuf = kb % 2
                          dma("pool", hb[kbuf][:], cbk[su, kb * 128:(kb + 1) * 128, :], writes=[("hb", kbuf)])
                          if kb % 2 == 0:
                              transpose_tile(hb[kbuf], ("hb", kbuf), KTB[:, :, kb * 128:(kb + 1) * 128], ("KT", kb), "act")
                          else:
                              jtranspose_tile(hb[kbuf], ("hb", kbuf), KTB[:, :, kb * 128:(kb + 1) * 128], ("KT", kb))
                          dma("pool", VB[:, kb, :], cbv[su, kb * 128:(kb + 1) * 128, :], writes=[("VB", kb)])
                          if kb % 2 == 1:
                              jreverse_rows(VB[:, kb, :], ("VB", kb))
                  else:
                      qi = unit[1]
                      rows = 128
                      x1src = x1p[p, qi * 128:(qi + 1) * 128, :]
                  xa, xb = uiB % 2, 1 - uiB % 2
                  if uiB == 0:
                      dma("sp", xt[xa][:], x1src, writes=[("xt", xa)])
                  if uiB == 0:
                      rms_stats(xt[xa], ("xt", xa))
                  pre = uiB > 0
                  norm_to(xt[xa], ("xt", xa), gbc[0], "gbc0", hb[0][:], ("hb", 0), alt=pre)
                  norm_to(xt[xa], ("xt", xa), gbc[1], "gbc1", hb[1][:], ("hb", 1), alt=pre)
                  transpose_tile(hb[0], ("hb", 0), hkvT, "hkvT", "dve")
                  transpose_tile(hb[1], ("hb", 1), hbT, "hbT", "act")
                  for ncn in range(2):
                      b = gemm_tm(hkvT, "hkvT", 0, ncn * 512, WK[0])
                      evac(stage[0][:, ncn * 512:(ncn + 1) * 512], "stage0", b, eng="act")
                      S.add("dve", lambda e, ncn=ncn: e.tensor_copy(
                          out=hb[0][:, ncn * 512:(ncn + 1) * 512], in_=stage[0][:, ncn * 512:(ncn + 1) * 512]),
                          reads=["stage0"], writes=[("hb", 0)])
                  if qi % 2 == 0:
                      transpose_tile(hb[0], ("hb", 0), KTB[:, :, qi * 128:(qi + 1) * 128], ("KT", qi), "act")
                  else:
                      jtranspose_tile(hb[0], ("hb", 0), KTB[:, :, qi * 128:(qi + 1) * 128], ("KT", qi))
                  if is_s:
                      dma("sp", nbk_s[su], stage[0][0:64, :], reads=["stage0"])
                  else:
                      dma("sp", nbk_p[p, qi * 128:(qi + 1) * 128, :], stage[0][:], reads=["stage0"])
                  for ncn in range(2):
                      b = gemm_tm(hkvT, "hkvT", 0, D + ncn * 512, WK[1])
                      evac(xt[xb][:, ncn * 512:(ncn + 1) * 512], ("xt", xb), b, eng="act")
                      S.add("dve", lambda e, ncn=ncn, qi=qi, xb=xb: e.tensor_copy(
                          out=VB[:, qi, ncn * 512:(ncn + 1) * 512], in_=xt[xb][:, ncn * 512:(ncn + 1) * 512]),
                          reads=[("xt", xb)], writes=[("VB", qi)])
                  if is_s:
                      dma("sp", nbv_s[su], xt[xb][0:64, :], reads=[("xt", xb)])
                  else:
                      dma("sp", nbv_p[p, qi * 128:(qi + 1) * 128, :], xt[xb][:], reads=[("xt", xb)])
                  if uiB + 1 < len(unitsB):
                      nu = unitsB[uiB + 1]
                      nsrc = x1s[nu[1]] if nu[0] == "s" else x1p[p, nu[1] * 128:(nu[1] + 1) * 128, :]
                      dma("sp", xt[xb][:], nsrc, writes=[("xt", xb)])
                  if qi % 2 == 1:
                      jreverse_rows(VB[:, qi, :], ("VB", qi))
                  for ncn in range(2):
                      b = gemm_tm(hbT, "hbT", 0, 2 * D + ncn * 512, WK[2])
                      S.add("dve", lambda e, b=b, ncn=ncn: e.tensor_scalar(
                          out=hb[1][:, ncn * 512:(ncn + 1) * 512], in0=psg[b][:, :],
                          scalar1=-SCALE, scalar2=None, op0=ALU.mult),
                          reads=[("psg", b)], writes=[("hb", 1)])
                  transpose_tile(hb[1], ("hb", 1), nQT, "nQT", "dve")
                  for ncn in range(2):
                      b = gemm_tm(hbT, "hbT", 0, 3 * D + ncn * 512, WK[3])
                      evac(hb[1][:, ncn * 512:(ncn + 1) * 512], ("hb", 1), b, func=AF.Silu)

                  chk(8)
                  if uiB == len(unitsB) - 1 and p + 1 < NPASS:
                      load_w(0, w_in_a[:, :], 4 * D, WK[0:4])
                  def run_sb(qi, is_s):
                      steps = []
                      cbanks = [(psc[0], ("psc", 0)), (psc[1], ("psc", 1)),
                                (pst[0][:, :].bitcast(F32), ("pst", 0)), (pst[1][:, :].bitcast(F32), ("pst", 1))]
                      for kb in range(qi, -1, -1):
                          for ob in range(2):
                              for par in range(2):
                                  heads = [ob * 8 + 2 * a_ + par for a_ in range(4)]
                                  steps.append(dict(ob=ob, par=par, heads=heads, kb=kb, diag=(kb == qi),
                                                    last=(kb == 0), zb=len(steps) % 2, eb=len(steps) % 3, cb=cbanks[ob * 2 + par]))

                      cw = 64 if is_s else 128
                      nw = 4 * cw

                      def hv(t_, w_=None):
                          w_ = cw if w_ is None else w_
                          return t_[:, 0:4 * w_].rearrange("p (a c) -> p a c", a=4)[:, :, 0:cw]

                      def ekeys(sp_):
                          eb = sp_["eb"]
                          return F6[2 * eb], F6[2 * eb + 1], ("F6", 2 * eb), ("F6", 2 * eb + 1)

                      def stageA1(sp_):
                          zb, kb, heads, diag = sp_["zb"], sp_["kb"], sp_["heads"], sp_["diag"]
                          E_, SP_, ek, sk = ekeys(sp_)
                          for j in range(4):
                              h = heads[j]
                              fp = h // 2
                              r0 = (h % 2) * 64
                              S.add("pe", lambda e, zb=zb, j=j, fp=fp, r0=r0, kb=kb: e.matmul(
                                  pss[zb][:, j * cw:(j + 1) * cw], lhsT=KTB[r0:r0 + 64, fp, kb * 128:(kb + 1) * 128],
                                  rhs=nQT[r0:r0 + 64, fp, 0:cw], start=True, stop=True),
                                  reads=[("KT", kb), "nQT"], writes=[("pss", zb)])
                          S.add("act", lambda e, zb=zb, E_=E_: e.activation(out=E_[:, 0:nw], in_=pss[zb][:, 0:nw], func=AF.Exp, scale=-1.0),
                                reads=[("pss", zb)], writes=[ek])
                          S.add("act", lambda e, zb=zb, E_=E_, SP_=SP_: e.activation(out=SP_[:, 0:nw], in_=E_[:, 0:nw], func=AF.Ln, bias=1.0),
                                reads=[ek], writes=[sk])

                      def stageA2(sp_):
                          zb, kb, heads, diag = sp_["zb"], sp_["kb"], sp_["heads"], sp_["diag"]
                          cbt, cbk = sp_["cb"]
                          E_, SP_, ek, sk = ekeys(sp_)
                          rev = (kb % 2 == 1)
                          if diag:
                              mk, mkey = (m01R, "m01R") if rev else (m01, "m01")
                              S.add(POOLENG, lambda e, SP_=SP_, mk=mk: e.tensor_tensor(out=hv(SP_), in0=hv(SP_), in1=hv(mk, 128), op=ALU.mult),
                                    reads=[sk, mkey], writes=[sk])
                              S.add(POOLENG, lambda e, E_=E_, mk=mk: e.tensor_tensor(out=hv(E_), in0=hv(E_), in1=hv(mk, 128), op=ALU.mult),
                                    reads=[ek, mkey], writes=[ek])
                          else:
                              if rev:
                                  S.add("dve", lambda e, cbt=cbt, SP_=SP_: e.tensor_tensor(
                                      out=SP_[0:1, 0:nw], in0=cbt[0:1, 0:nw], in1=SP_[0:1, 0:nw], op=ALU.add),
                                      reads=[cbk, sk], writes=[sk])
                              else:
                                  S.add("dve", lambda e, cbt=cbt, SP_=SP_: e.scalar_tensor_tensor(
                                      out=SP_[96:128, 0:nw], in0=cbt[96:128, 0:nw], scalar=oh127[96:128, :],
                                      in1=SP_[96:128, 0:nw], op0=ALU.mult, op1=ALU.add),
                                      reads=[cbk, sk, "oh127"], writes=[sk])
                          S.add("dve", lambda e, zb=zb, SP_=SP_: e.tensor_copy(out=HI[zb][:, 0:nw], in_=SP_[:, 0:nw]),
                                reads=[sk], writes=[("HI", zb)])
                          S.add(POOLENG, lambda e, zb=zb, SP_=SP_: e.tensor_tensor(out=LO[zb][:, 0:nw], in0=SP_[:, 0:nw], in1=HI[zb][:, 0:nw], op=ALU.subtract),
                                reads=[sk, ("HI", zb)], writes=[("LO", zb)])

                      def stageB(sp_):
                          zb, kb, diag = sp_["zb"], sp_["kb"], sp_["diag"]
                          cbt, cbk = sp_["cb"]
                          E_, SP_, ek, sk = ekeys(sp_)
                          Lt, lkey = (LmR, "LmR") if (kb % 2 == 1) else (Lm, "Lm")
                          S.add("pe", lambda e, zb=zb, Lt=Lt, cbt=cbt: e.matmul(cbt[:, 0:nw], lhsT=Lt[:], rhs=HI[zb][:, 0:nw], start=True, stop=False),
                                reads=[lkey, ("HI", zb)], writes=[cbk])
                          S.add("pe", lambda e, zb=zb, Lt=Lt, cbt=cbt: e.matmul(cbt[:, 0:nw], lhsT=Lt[:], rhs=LO[zb][:, 0:nw], start=False, stop=True),
                                reads=[lkey, ("LO", zb)], writes=[cbk])
                          S.add("act", lambda e, cbt=cbt, SP_=SP_: e.activation(out=SP_[:, 0:nw], in_=cbt[:, 0:nw], func=AF.Exp, scale=-1.0),
                                reads=[cbk, ("HI", zb), ("LO", zb)], writes=[sk])

                      def stageB2(sp_):
                          zb = sp_["zb"]
                          E_, SP_, ek, sk = ekeys(sp_)
                          S.add("dve", lambda e, zb=zb, E_=E_, SP_=SP_: e.tensor_tensor(out=PB[zb][:, 0:nw], in0=E_[:, 0:nw], in1=SP_[:, 0:nw], op=ALU.mult),
                                reads=[ek, sk], writes=[("PB", zb)])

                      def stageC(sp_):
                          zb, kb, diag, heads, ob, par = sp_["zb"], sp_["kb"], sp_["diag"], sp_["heads"], sp_["ob"], sp_["par"]
                          for j in range(4):
                              h = heads[j]
                              oc = (h % 8) * 64
                              S.add("pe", lambda e, zb=zb, j=j, h=h, oc=oc, ob=ob, kb=kb, diag=diag, par=par: e.matmul(
                                  psg[ob][0:cw, oc:oc + 64], lhsT=PB[zb][:, j * cw:(j + 1) * cw],
                                  rhs=VB[:, kb, h * 64:(h + 1) * 64], start=(diag and j == 0 and par == 0), stop=(kb == 0),
                                  skip_group_check=True),
                                  reads=[("PB", zb), ("VB", kb)], writes=[("psg", ob)])
                          for _jk in range(JUNK):
                              S.add("pe", lambda e, ob=ob, kb=kb: e.matmul(
                                  psg[ob][0:cw, :], lhsT=Zm[:, 0:cw], rhs=VB[:, kb, ob * 512:(ob + 1) * 512],
                                  start=False, stop=(kb == 0), skip_group_check=True),
                                  reads=["Zm", ("VB", kb)], writes=[("psg", ob)])
                          if sp_["last"]:
                              def hview(t_, par=par):
                                  return t_.rearrange("p (a t e) -> p a t e", a=4, t=2)[:, :, par, :]
                              S.add("dve", lambda e, ob=ob, hview=hview: e.tensor_tensor(
                                  out=hview(hb[0][0:cw, ob * 512:(ob + 1) * 512]), in0=hview(psg[ob][0:cw, :]),
                                  in1=hview(hb[1][0:cw, ob * 512:(ob + 1) * 512]), op=ALU.mult),
                                  reads=[("psg", ob), ("hb", 1)], writes=[("hb", 0)])

                      nst = len(steps)
                      stageA1(steps[0])
                      if nst > 1:
                          stageA1(steps[1])
                      stageA2(steps[0])
                      for s_i in range(nst):
                          stageB(steps[s_i])
                          if s_i + 2 < nst:
                              stageA1(steps[s_i + 2])
                          if s_i + 1 < nst:
                              stageA2(steps[s_i + 1])
                          stageB2(steps[s_i])
                          if s_i >= 1:
                              stageC(steps[s_i - 1])
                      stageC(steps[nst - 1])
                  run_sb(qi, is_s)
                  if uiB + 1 < len(unitsB):
                      rms_stats(xt[xb], ("xt", xb), alt=True)
                  chk(9)
                  transpose_tile(hb[0], ("hb", 0), ogT[:, :, :], "ogT", "dve")
                  for ncn in range(2):
                      b = gemm_tm(ogT, "ogT", 0, 4 * D + ncn * 512, WK[4])
                      S.add("dve", lambda e, b=b, ncn=ncn, xa=xa: e.tensor_tensor(
                          out=xt[xa][:, ncn * 512:(ncn + 1) * 512], in0=psg[b][:, :],
                          in1=xt[xa][:, ncn * 512:(ncn + 1) * 512], op=ALU.add),
                          reads=[("psg", b), ("xt", xa)], writes=[("xt", xa)])
                  rms_stats(xt[xa], ("xt", xa))
                  norm_to(xt[xa], ("xt", xa), gbc[2], "gbc2", stage[0][:], "stage0")
                  if is_s:
                      dma("sp", y_s[su], stage[0][0:64, :], reads=["stage0"])
                  else:
                      dma("sp", y_p[p, qi * 128:(qi + 1) * 128, :], stage[0][:], reads=["stage0"])
                  if uiB == len(unitsB) - 1 and p + 1 < NPASS:
                      load_w(4 * D, w_out_a[:, :], D, WK[4:5])

        try:
            body()
        except _Stop:
            S.barrier()
        S.emit(sems, dsems, block)
    return nc


def host_consts():
    ident = np.eye(128, dtype=np.float32)
    j = np.arange(128)[:, None]
    s = np.arange(128)[None, :]
    Lmat = (j >= s).astype(np.float32)
    m01 = (j < s).astype(np.float32)
    J = np.ascontiguousarray(ident[::-1])
    return np.concatenate([ident, Lmat, np.tile(m01, (1, 4)), J], axis=1)


def host_consts2():
    p = np.arange(128)[:, None]
    q = np.arange(128)[None, :]
    LR = (p <= q).astype(np.float32)
    mR = ((127 - p) < q).astype(np.float32)
    oh = np.zeros((128, 64), np.float32)
    oh[127, :] = 1.0
    return np.concatenate([LR, np.tile(mR, (1, 4)), oh], axis=1)


def host_ttab(rel_bias):
    ik = np.arange(128)[:, None]
    m = np.arange(640)[None, :]
    idx = np.clip(m - ik, -128, 128) + 128
    T = np.ascontiguousarray(rel_bias[:, idx]).astype(np.float32)
    T[:, 64:128, 0:64] = NEG
    T[:, 0:64, 576:640] = NEG
    return T


def make_in_maps(inp, SEQ, PAST, NPASS, ncores):
    NS = 2 * NPASS
    g128 = np.stack([np.broadcast_to(v, (128, D)) for v in
                     (inp["norm_a"][0], inp["norm_kv"], inp["norm_b"][0], inp["norm_f"])]).astype(np.float32)
    g128 = np.ascontiguousarray(g128)
    T = host_ttab(np.asarray(inp["rel_bias_a"][0]))
    cst = host_consts()
    cst2 = host_consts2()
    shared = {
        "w_in_a": np.ascontiguousarray(inp["w_in_a"][0]), "w_out_a": np.ascontiguousarray(inp["w_out_a"][0]),
        "w_kv": np.ascontiguousarray(inp["w_kv"]), "w_in_b": np.ascontiguousarray(inp["w_in_b"][0]),
        "w_out_b": np.ascontiguousarray(inp["w_out_b"][0]), "g128": g128, "ttab": T, "cst": cst, "cst2": cst2,
    }
    maps = []
    for c in range(ncores):
        m = dict(shared)
        m["xp"] = np.ascontiguousarray(inp["x_prompt"][c * NPASS:(c + 1) * NPASS])
        m["xs"] = np.ascontiguousarray(inp["x_sample"][c * NS:(c + 1) * NS])
        m["cak"] = np.ascontiguousarray(inp["cache_a_k"][0, c * NS:(c + 1) * NS]).reshape(NS, 512, D)
        m["cav"] = np.ascontiguousarray(inp["cache_a_v"][0, c * NS:(c + 1) * NS]).reshape(NS, 512, D)
        m["cbk"] = np.ascontiguousarray(inp["cache_b_k"][c * NS:(c + 1) * NS]).reshape(NS, PAST, D)
        m["cbv"] = np.ascontiguousarray(inp["cache_b_v"][c * NS:(c + 1) * NS]).reshape(NS, PAST, D)
        maps.append(m)
    return maps


def assemble(results, SEQ, NPASS):
    def cat(name):
        return np.concatenate([r[name] for r in results], axis=0)
    B = len(results) * NPASS
    NSs = 2 * B
    y_p = cat("y_p")
    y_s = cat("y_s")
    nak_p = cat("nak_p").reshape(1, B, 512, H, DH)
    nav_p = cat("nav_p").reshape(1, B, 512, H, DH)
    nbk_p = cat("nbk_p").reshape(B, SEQ, H, DH)
    nbv_p = cat("nbv_p").reshape(B, SEQ, H, DH)
    nak_s = cat("nak_s").reshape(1, NSs, 64, H, DH)
    nav_s = cat("nav_s").reshape(1, NSs, 64, H, DH)
    nbk_s = cat("nbk_s").reshape(NSs, 64, H, DH)
    nbv_s = cat("nbv_s").reshape(NSs, 64, H, DH)
    return (y_p, y_s, nak_p, nav_p, nbk_p, nbv_p, nak_s, nav_s, nbk_s, nbv_s)


def kernel(**inputs):
    inp = {k: np.asarray(v) for k, v in inputs.items()}
    SEQ = inp["x_prompt"].shape[1]
    PAST = inp["cache_b_k"].shape[1]
    NPASS = inp["x_prompt"].shape[0] // NCORES
    nc = build_program(SEQ, PAST, NPASS)
    maps = make_in_maps(inp, SEQ, PAST, NPASS, NCORES)
    res = run_bass_kernel_spmd(nc, maps, core_ids=list(range(NCORES)))
    outs = assemble(res.results, SEQ, NPASS)
    return tuple(np.ascontiguousarray(o, dtype=np.float32) for o in outs)
```
